# Optimizing a Trainium2 kernel written in Bass

```python
import math, functools
import jax, jax.numpy as jnp
from jax import lax
import numpy as np

D_MODEL = 1024
BATCH = 2
SEQ = 8192
DEPTH = 1
DEC_BATCH = 32
DEC_SEQ = 32
PAST_LEN = 4096

CHUNK = 64
HEAD_DIM = 64
N_HEADS_A = 8
BAND_CHUNKS_A = 8
REACH_A = BAND_CHUNKS_A * CHUNK
REL_CLIP_A = 128
N_HEADS_B = 8
N_KV_B = 2
GROUP_B = N_HEADS_B // N_KV_B
WINDOW_B = 128
BAND_CHUNKS_B = WINDOW_B // CHUNK
REACH_B = BAND_CHUNKS_B * CHUNK
N_BUCKETS = 32
T5_MAX_DIST = 128
WIDTH_A = N_HEADS_A * HEAD_DIM
WIDTH_B = N_HEADS_B * HEAD_DIM
KV_WIDTH_B = N_KV_B * HEAD_DIM
MIX_WIDTH = WIDTH_A + WIDTH_B
IN_WIDTH = 3 * WIDTH_A + WIDTH_B + 2 * KV_WIDTH_B
SPLITS = (WIDTH_A, 2 * WIDTH_A, 3 * WIDTH_A, 3 * WIDTH_A + WIDTH_B, 3 * WIDTH_A + WIDTH_B + KV_WIDTH_B)
D_FF = 2816
N_MOD = 9
FFN_RES = 0.5
EPS = 1e-6
SCALE = HEAD_DIM ** -0.5
NEG_INF = -1e30

kernel_name = "hybrid_streaming_encoder_step"


def rmsnorm(x, g):
    xf = x.astype(jnp.float32)
    y = xf * lax.rsqrt(jnp.mean(xf * xf, axis=-1, keepdims=True) + EPS)
    return (y * g.astype(jnp.float32)).astype(x.dtype)


def adaln_in(x, g, shift, scale):
    return rmsnorm(x, g) * (1 + scale[:, None, :]) + shift[:, None, :]


def swiglu(h, w_gate, w_up, w_down):
    return (jax.nn.silu(h @ w_gate) * (h @ w_up)) @ w_down


def rel_k_minus_q(q_off, n_q, n_k):
    return jnp.arange(n_k)[None, :] - (jnp.arange(n_q) + q_off)[:, None]


def clipped_rel_bias(table, rel):
    return table[:, jnp.clip(rel, -REL_CLIP_A, REL_CLIP_A) + REL_CLIP_A]


def t5_bucket(rel):
    half = N_BUCKETS // 2
    max_exact = half // 2
    n = jnp.abs(rel)
    base = jnp.where(rel > 0, half, 0)
    nf = jnp.maximum(n, 1).astype(jnp.float32)
    large = max_exact + (jnp.log(nf / max_exact) / math.log(T5_MAX_DIST / max_exact)
                         * (half - max_exact)).astype(jnp.int32)
    large = jnp.minimum(large, half - 1)
    return base + jnp.where(n < max_exact, n, large)


def t5_rel_bias(table, rel):
    return table[:, t5_bucket(rel)]


def gather_band(t, n_prev):
    B, T = t.shape[:2]
    nc = T // CHUNK
    tc = t.reshape((B, nc, CHUNK) + t.shape[2:])
    tp = jnp.pad(tc, [(0, 0), (n_prev, 0)] + [(0, 0)] * (tc.ndim - 2))
    idx = jnp.arange(nc)[:, None] + jnp.arange(n_prev + 1)[None, :]
    band = tp[:, idx]
    return band.reshape((B, nc, (n_prev + 1) * CHUNK) + t.shape[2:])


def band_valid(nc, n_prev):
    k_pos = (jnp.arange(nc)[:, None] - n_prev) * CHUNK + jnp.arange((n_prev + 1) * CHUNK)[None, :]
    return (k_pos >= 0)[:, None, :]


def band_attn_a(q, k, v, bias, valid):
    s = jnp.einsum('bnqhd,bnkhd->bnhqk', q, k).astype(jnp.float32) * SCALE + bias.astype(jnp.float32)
    s = jnp.where(valid[None, :, None], s, NEG_INF)
    p = jax.nn.softmax(s, axis=-1).astype(v.dtype)
    return jnp.einsum('bnhqk,bnkhd->bnqhd', p, v)


def window_attn_b(q, k, v, bias, sinks, valid):
    n_q, n_k = bias.shape[1:]
    s = jnp.einsum('bnqcgd,bnkcd->bncgqk', q, k).astype(jnp.float32) * SCALE \
        + bias.reshape(N_KV_B, GROUP_B, n_q, n_k).astype(jnp.float32)
    s = jnp.where(valid[None, :, None, None], s, NEG_INF)
    sink = sinks.astype(jnp.float32).reshape(N_KV_B, GROUP_B, 1, 1)
    m = jnp.maximum(jnp.max(s, axis=-1, keepdims=True), sink)
    e = jnp.exp(s - m)
    p = e / (jnp.sum(e, axis=-1, keepdims=True) + jnp.exp(sink - m))
    return jnp.einsum('bncgqk,bnkcd->bnqcgd', p.astype(v.dtype), v)


def mixers_prompt(qa, ka, va, qb, kb, vb, rel_tab_a, t5_tab, sinks_b):
    B, T = qa.shape[:2]
    nc = T // CHUNK
    len_a = (BAND_CHUNKS_A + 1) * CHUNK
    bias_a = clipped_rel_bias(rel_tab_a, rel_k_minus_q(BAND_CHUNKS_A * CHUNK, CHUNK, len_a))
    oa = band_attn_a(qa.reshape(B, nc, CHUNK, N_HEADS_A, HEAD_DIM),
                     gather_band(ka, BAND_CHUNKS_A), gather_band(va, BAND_CHUNKS_A),
                     bias_a, band_valid(nc, BAND_CHUNKS_A))
    len_b = (BAND_CHUNKS_B + 1) * CHUNK
    bias_b = t5_rel_bias(t5_tab, rel_k_minus_q(BAND_CHUNKS_B * CHUNK, CHUNK, len_b))
    ob = window_attn_b(qb.reshape(B, nc, CHUNK, N_KV_B, GROUP_B, HEAD_DIM),
                       gather_band(kb, BAND_CHUNKS_B), gather_band(vb, BAND_CHUNKS_B),
                       bias_b, sinks_b, band_valid(nc, BAND_CHUNKS_B))
    la, lb = min(REACH_A, T), min(REACH_B, T)
    rows = (ka[:, T - la:], va[:, T - la:], kb[:, T - lb:], vb[:, T - lb:])
    return oa.reshape(B, T, WIDTH_A), ob.reshape(B, T, WIDTH_B), rows


def mixers_sample(qa, ka, va, qb, kb, vb, cache_ak, cache_av, cache_bk, cache_bv, rel_tab_a, t5_tab, sinks_b):
    B, S = qa.shape[:2]
    la, lb = cache_ak.shape[1], cache_bk.shape[1]
    ka_all = jnp.concatenate([cache_ak.astype(ka.dtype), ka], axis=1)[:, None]
    va_all = jnp.concatenate([cache_av.astype(va.dtype), va], axis=1)[:, None]
    bias_a = clipped_rel_bias(rel_tab_a, rel_k_minus_q(la, S, la + S))
    oa = band_attn_a(qa[:, None], ka_all, va_all, bias_a, jnp.ones((1, 1, la + S), bool))[:, 0]
    kb_all = jnp.concatenate([cache_bk.astype(kb.dtype), kb], axis=1)[:, None]
    vb_all = jnp.concatenate([cache_bv.astype(vb.dtype), vb], axis=1)[:, None]
    bias_b = t5_rel_bias(t5_tab, rel_k_minus_q(lb, S, lb + S))
    ob = window_attn_b(qb.reshape(B, 1, S, N_KV_B, GROUP_B, HEAD_DIM), kb_all, vb_all,
                       bias_b, sinks_b, jnp.ones((1, 1, lb + S), bool))[:, 0]
    return oa.reshape(B, S, WIDTH_A), ob.reshape(B, S, WIDTH_B), (ka, va, kb, vb)


def layer(x, c, w_mod, b_mod, gains, w1_gate, w1_up, w1_down, w_in, w_out, group_gains,
          w2_gate, w2_up, w2_down, mixers):
    mod = jax.nn.silu(c) @ w_mod + b_mod
    sh1, sc1, g1, sh2, sc2, g2, sh3, sc3, g3 = jnp.split(mod, N_MOD, axis=-1)
    h = adaln_in(x, gains[0], sh1, sc1)
    x = x + FFN_RES * g1[:, None] * rmsnorm(swiglu(h, w1_gate, w1_up, w1_down), gains[1])
    h = adaln_in(x, gains[2], sh2, sc2)
    B, T = h.shape[:2]
    qa, ka, va, qb, kb, vb = jnp.split(h @ w_in, SPLITS, axis=-1)
    qa = qa.reshape(B, T, N_HEADS_A, HEAD_DIM)
    ka = ka.reshape(B, T, N_HEADS_A, HEAD_DIM)
    va = va.reshape(B, T, N_HEADS_A, HEAD_DIM)
    qb = qb.reshape(B, T, N_HEADS_B, HEAD_DIM)
    kb = kb.reshape(B, T, N_KV_B, HEAD_DIM)
    vb = vb.reshape(B, T, N_KV_B, HEAD_DIM)
    oa, ob, rows = mixers(qa, ka, va, qb, kb, vb)
    o = jnp.concatenate([rmsnorm(oa, group_gains[:WIDTH_A]), rmsnorm(ob, group_gains[WIDTH_A:])], axis=-1) @ w_out
    x = x + g2[:, None] * rmsnorm(o, gains[3])
    h = adaln_in(x, gains[4], sh3, sc3)
    x = x + FFN_RES * g3[:, None] * rmsnorm(swiglu(h, w2_gate, w2_up, w2_down), gains[5])
    return x, rows


def setup_inputs(seed: int = 0) -> dict:
    key = jax.random.key(seed)
    ks = iter(jax.random.split(key, 32))

    def nrm(shape, s):
        return jax.random.normal(next(ks), shape, jnp.float32) * s

    la, lb = min(REACH_A, PAST_LEN), min(REACH_B, PAST_LEN)
    return {
        "x_prompt": nrm((BATCH, SEQ, D_MODEL), 1.0),
        "x_sample": nrm((DEC_BATCH, DEC_SEQ, D_MODEL), 1.0),
        "cache_a_k": nrm((DEPTH, DEC_BATCH, la, N_HEADS_A, HEAD_DIM), 1.0),
        "cache_a_v": nrm((DEPTH, DEC_BATCH, la, N_HEADS_A, HEAD_DIM), 1.0),
        "cache_b_k": nrm((DEPTH, DEC_BATCH, lb, N_KV_B, HEAD_DIM), 1.0),
        "cache_b_v": nrm((DEPTH, DEC_BATCH, lb, N_KV_B, HEAD_DIM), 1.0),
        "c_prompt": nrm((BATCH, D_MODEL), 1.0),
        "c_sample": nrm((DEC_BATCH, D_MODEL), 1.0),
        "w_mod": nrm((DEPTH, D_MODEL, N_MOD * D_MODEL), D_MODEL ** -0.5),
        "b_mod": nrm((DEPTH, N_MOD * D_MODEL), 0.02),
        "norm_gains": 1.0 + nrm((DEPTH, 6, D_MODEL), 0.05),
        "w1_gate": nrm((DEPTH, D_MODEL, D_FF), D_MODEL ** -0.5),
        "w1_up": nrm((DEPTH, D_MODEL, D_FF), D_MODEL ** -0.5),
        "w1_down": nrm((DEPTH, D_FF, D_MODEL), D_FF ** -0.5),
        "w_in": nrm((DEPTH, D_MODEL, IN_WIDTH), D_MODEL ** -0.5),
        "w_out": nrm((DEPTH, MIX_WIDTH, D_MODEL), MIX_WIDTH ** -0.5),
        "group_gains": 1.0 + nrm((DEPTH, MIX_WIDTH), 0.05),
        "rel_bias_a": nrm((DEPTH, N_HEADS_A, 2 * REL_CLIP_A + 1), 0.2),
        "t5_bias_table": nrm((N_HEADS_B, N_BUCKETS), 0.2),
        "sinks_b": nrm((DEPTH, N_HEADS_B), 1.0),
        "w2_gate": nrm((DEPTH, D_MODEL, D_FF), D_MODEL ** -0.5),
        "w2_up": nrm((DEPTH, D_MODEL, D_FF), D_MODEL ** -0.5),
        "w2_down": nrm((DEPTH, D_FF, D_MODEL), D_FF ** -0.5),
    }


def reference(x_prompt, x_sample, cache_a_k, cache_a_v, cache_b_k, cache_b_v, c_prompt, c_sample,
              w_mod, b_mod, norm_gains, w1_gate, w1_up, w1_down, w_in, w_out, group_gains,
              rel_bias_a, t5_bias_table, sinks_b, w2_gate, w2_up, w2_down):
    xp, xs = x_prompt, x_sample
    rows_p, rows_s = [], []
    for l in range(DEPTH):
        weights = (w_mod[l], b_mod[l], norm_gains[l], w1_gate[l], w1_up[l], w1_down[l], w_in[l], w_out[l],
                   group_gains[l], w2_gate[l], w2_up[l], w2_down[l])
        mix_p = functools.partial(mixers_prompt, rel_tab_a=rel_bias_a[l], t5_tab=t5_bias_table,
                                  sinks_b=sinks_b[l])
        mix_s = functools.partial(mixers_sample, cache_ak=cache_a_k[l], cache_av=cache_a_v[l],
                                  cache_bk=cache_b_k[l], cache_bv=cache_b_v[l], rel_tab_a=rel_bias_a[l],
                                  t5_tab=t5_bias_table, sinks_b=sinks_b[l])
        xp, rp = layer(xp, c_prompt, *weights, mix_p)
        xs, rs = layer(xs, c_sample, *weights, mix_s)
        rows_p.append(rp)
        rows_s.append(rs)
    new_a_k_prompt = jnp.stack([r[0] for r in rows_p])
    new_a_v_prompt = jnp.stack([r[1] for r in rows_p])
    new_b_k_prompt = jnp.stack([r[2] for r in rows_p])
    new_b_v_prompt = jnp.stack([r[3] for r in rows_p])
    new_a_k_sample = jnp.stack([r[0] for r in rows_s])
    new_a_v_sample = jnp.stack([r[1] for r in rows_s])
    new_b_k_sample = jnp.stack([r[2] for r in rows_s])
    new_b_v_sample = jnp.stack([r[3] for r in rows_s])
    return (xp, xs, new_a_k_prompt, new_a_v_prompt, new_b_k_prompt, new_b_v_prompt,
            new_a_k_sample, new_a_v_sample, new_b_k_sample, new_b_v_sample)
```

```python
import numpy as np
from contextlib import ExitStack
import concourse.bass as bass
import concourse.mybir as mybir
from concourse.bass_utils import run_bass_kernel_spmd

F32 = mybir.dt.float32
BF16 = mybir.dt.bfloat16
AF = mybir.ActivationFunctionType
ALU = mybir.AluOpType

NCORES = 8
D = 1024
DFF = 2816
NJ = 22
SUPS = [6, 6, 6, 4]
JS = 6
NSLOT = 4
SCALE = 0.125
EPS = 1e-6
NEG = -1e30
NROWS = 2688


class Buf:
    __slots__ = ("name", "last_w", "reads", "psum")

    def __init__(self, name, psum=False):
        self.name = name
        self.last_w = None
        self.reads = []
        self.psum = psum


class Sched:
    ENGS = ("pe", "act", "dve", "pool", "sp")

    def __init__(self, nc, es):
        self.nc = nc
        self.es = es
        self.streams = {e: [] for e in self.ENGS}
        self.sems = {}
        self.count = {}
        self.seen = {e: {} for e in self.ENGS}
        self.dead = False
        for e in self.ENGS:
            self.new_sem("E_" + e)

    def new_sem(self, name):
        s = self.es.enter_context(self.nc.semaphore(name))
        self.sems[name] = s
        self.count[name] = 0
        return name

    def _deps(self, eng, reads, writes):
        deps = {}
        own = "E_" + eng

        def add(t, psum=False):
            if t is None:
                return
            s, v = t
            if psum and s == own:
                return
            if deps.get(s, 0) < v:
                deps[s] = v
        for b in reads:
            add(b.last_w, b.psum)
            if b.psum:
                for r in b.reads:
                    add(r, True)
        for b in writes:
            add(b.last_w, b.psum)
            for r in b.reads:
                add(r, b.psum)
        waits = []
        seen = self.seen[eng]
        for s, v in deps.items():
            if eng == "pe" and s == "E_pe":
                continue
            if seen.get(s, 0) >= v:
                continue
            seen[s] = v
            waits.append((s, v))
        return waits

    def _finish(self, tok, reads, writes):
        for b in writes:
            b.last_w = tok
            b.reads = []
        for b in reads:
            if b not in writes:
                b.reads.append(tok)
                if len(b.reads) > 48:
                    m = {}
                    for s, v in b.reads:
                        if m.get(s, 0) < v:
                            m[s] = v
                    b.reads = list(m.items())

    def op(self, eng, fn, reads=(), writes=()):
        if self.dead:
            return None
        reads = [b for b in reads if b is not None]
        writes = [b for b in writes if b is not None]
        waits = self._deps(eng, reads, writes)
        sname = "E_" + eng
        self.count[sname] += 1
        tok = (sname, self.count[sname])
        self.streams[eng].append((waits, fn, sname, 1))
        self._finish(tok, reads, writes)
        return tok

    def dma(self, eng, sem, fns, reads=(), writes=()):
        if self.dead:
            return None
        reads = [b for b in reads if b is not None]
        writes = [b for b in writes if b is not None]
        waits = self._deps(eng, reads, writes)
        tok = None
        for i, fn in enumerate(fns):
            self.count[sem] += 16
            tok = (sem, self.count[sem])
            self.streams[eng].append((waits if i == 0 else [], fn, sem, 16))
        self._finish(tok, reads, writes)
        return tok

    def finish(self, eng):
        waits = [(s, v) for s, v in self.count.items() if v > 0]
        self.streams[eng].append((waits, None, None, 0))

    def build(self):
        nc = self.nc
        with nc.Block() as block:
            def mk(ename):
                def body(e):
                    for waits, fn, sname, inc in self.streams[ename]:
                        for s, v in waits:
                            e.wait_ge(self.sems[s], v)
                        if fn is not None:
                            fn(e).then_inc(self.sems[sname], inc)
                return body
            block.tensor(mk("pe"))
            block.scalar(mk("act"))
            block.vector(mk("dve"))
            block.gpsimd(mk("pool"))
            block.sync(mk("sp"))


def build_program():
    nc = bass.Bass("TRN2", target_bir_lowering=False)
    es = ExitStack()

    def din(name, shape):
        return nc.dram_tensor(name, list(shape), F32, kind="ExternalInput").ap()

    def dout(name, shape):
        return nc.dram_tensor(name, list(shape), F32, kind="ExternalOutput").ap()

    xin = din("xin", [NROWS, D])
    cvec = din("cvec", [5, D])
    hneg_d = din("hneg", [128, 1])
    ident_d = din("ident", [128, 128])
    w_mod = din("w_mod", [36, 128, 2048])
    b_mod = din("b_mod", [1, 9 * D])
    bmod_fm = din("bmod_fm", [128, 72])
    gains = din("gains", [6, D])
    gains_fm = din("gains_fm", [128, 48])
    w1g = din("w1_gate", [NJ, 128, 1024]); w1u = din("w1_up", [NJ, 128, 1024]); w1d = din("w1_down", [128, NJ, D])
    w2g = din("w2_gate", [NJ, 128, 1024]); w2u = din("w2_up", [NJ, 128, 1024]); w2d = din("w2_down", [128, NJ, D])
    w_in = din("w_in", [9, 128, 2048])
    w_out = din("w_out", [4, 128, 2048])
    ggain = din("ggain", [1, D])
    relrev = din("relrev", [8, 257])
    t5T = din("t5T", [32, 8])
    oh_d = din("oh", [32, 384])
    sinks = din("sinks", [1, 8])
    cak = din("cak", [4, 512, 512]); cav = din("cav", [4, 512, 512])
    cbk = din("cbk", [4, 128, 128]); cbv = din("cbv", [4, 128, 128])

    y_out = dout("y", [2176, D])
    kap = dout("kap", [512, 512]); vap = dout("vap", [512, 512])
    kbp = dout("kbp", [128, 128]); vbp = dout("vbp", [128, 128])
    kas = dout("kas", [128, 512]); vas = dout("vas", [128, 512])
    kbs = dout("kbs", [128, 128]); vbs = dout("vbs", [128, 128])

    g2a = nc.dram_tensor("g2a", [8, 128, 768], F32)
    g2b = nc.dram_tensor("g2b", [8, 128, 384], F32)
    wsc = nc.dram_tensor("wsc", [57, 128, 2048], BF16)
    wdsc = nc.dram_tensor("wdsc", [2, 128, NJ * D], BF16)

    with es:
        S = Sched(nc, es)

        def sb(name, shape, dt=F32):
            return es.enter_context(nc.sbuf_tensor(name, list(shape), dt))

        xr = [sb("x_res%d" % i, [128, 4, D]) for i in range(2)]; xBs = [[Buf("x%d_%d" % (j, i)) for i in range(4)] for j in range(2)]
        x_res = xr[0]; xB = xBs[0]
        X = {"t": xr[0], "B": xBs[0], "i": 0}
        y_all = sb("y_all", [128, 4, D]); yB = [Buf("y%d" % i) for i in range(4)]
        hT = sb("hT", [128, 8, 512], BF16); hBt = [[Buf("h%d_%d" % (i, t)) for t in range(4)] for i in range(8)]
        hB = [b_ for row in hBt for b_ in row]
        regA = sb("regA", [128, 8 * 512], BF16)
        aT = regA[:, :].rearrange("p (j t) -> p j t", j=8)
        aB = [Buf("a%d" % i) for i in range(8)]
        QAT = regA[:, 0:2048].rearrange("p (c t) -> p c t", c=4)
        QBT = regA[:, 2048:4096].rearrange("p (c t) -> p c t", c=4)
        qaB = aB[0:4]; qbB = aB[4:8]
        wd = sb("wd", [128, JS, D], BF16); wdB = Buf("wd")
        ring = sb("ring", [128, NSLOT, 2048], BF16); ringB = [Buf("r%d" % i) for i in range(NSLOT)]
        KAT = sb("KAT", [128, 4, 1024], BF16); kaB = [Buf("ka%d" % i) for i in range(8)]
        VA = sb("VA", [128, 8, 8, 65], BF16); vaB = [Buf("va%d" % i) for i in range(8)]
        KBT = sb("KBT", [128, 1024], BF16); kbB = [Buf("kb%d" % i) for i in range(8)]
        VB = sb("VB", [128, 8, 2, 65], BF16); vbB = [Buf("vb%d" % i) for i in range(8)]
        KnA = sb("KnA", [128, 4, 128], BF16); knaB = Buf("kna")
        KnB = sb("KnB", [128, 128], BF16); knbB = Buf("knb")
        VnA = sb("VnA", [32, 4, 8, 65], BF16); vnaB = Buf("vna")
        VnB = sb("VnB", [32, 4, 2, 65], BF16); vnbB = Buf("vnb")
        biasA = sb("biasA", [128, 5, 2, 4, 128], BF16); bAB = Buf("biasA")
        biasB = sb("biasB", [128, 2, 8, 128], BF16); bBB = Buf("biasB")
        PT = sb("PT", [128, 2, 5, 512], BF16); ptB = [[Buf("pt%d%d" % (a, k)) for k in range(5)] for a in range(2)]
        PTB = sb("PTB", [128, 2, 2, 512], BF16); ptbB = [[Buf("ptb%d%d" % (a, k)) for k in range(2)] for a in range(2)]
        sc = sb("sc", [128, 2, 512]); scB = [Buf("sc0"), Buf("sc1")]
        Cp = sb("Cp", [128, 3, D]); CpB = Buf("Cp")
        Cs = sb("Cs", [128, 3, D]); CsB = Buf("Cs")
        gg = sb("gg", [128, D]); ggB = Buf("gg")
        xn = sb("xn", [128, 2, D]); xnB = [Buf("xn0"), Buf("xn1")]
        tmp = sb("tmp", [128, D]); tmpB = Buf("tmp")
        bstg = tmp[:, :].rearrange("p (h q) -> p h q", h=8); bstgB = tmpB
        junk = sb("junk", [128, D], BF16); junkB = Buf("junk")
        sg = sb("sg", [128, 2, 512], BF16); sgB = [Buf("sg0"), Buf("sg1")]
        SpT = PT[:, 0, 0:2, :].rearrange("p a (b t) -> p (a b) t", b=4); SsT = PT[:, 1, 0:2, :].rearrange("p a (b t) -> p (a b) t", b=4)
        sptB = Buf("SpT"); sstB = Buf("SsT")
        S5 = sb("S5", [128, 8, 8], BF16); s5B = Buf("S5")
        ident = sb("ident_s", [128, 128]); identB = Buf("ident")
        identb = sb("identb", [128, 128], BF16); identbB = Buf("identb")
        kvst = sb("kvst", [128, 2, 512]); kvB = [Buf("kv0"), Buf("kv1")]
        cks = [wd[:, 0:2, :].rearrange("p a (b f) -> p (a b) f", b=2), wd[:, 3:5, :].rearrange("p a (b f) -> p (a b) f", b=2)]
        ckBs = [Buf("ck0"), Buf("ck1")]
        ckbs = [wd[:, 2, 0:128], wd[:, 2, 128:256]]; ckbBs = [Buf("ckb0"), Buf("ckb1")]
        modT = sb("modT", [128, 9, 8, 8]); modB = Buf("modT")
        Aall = sb("Aall", [128, 3, 8, 8]); AB = Buf("Aall")
        bmfm = sb("bmfm", [128, 72]); gfm = sb("gfm", [128, 48]); smallB = Buf("small")
        hneg = sb("hneg_s", [128, 1])
        eps_t = sb("eps_t", [128, 1])
        esink = sb("esink", [128, 8])
        st = sb("st", [128, 96]); stB = Buf("st")
        stpre = [Buf("stpre%d" % i) for i in range(4)]; stpost = [Buf("stpost%d" % i) for i in range(4)]
        stgn = [Buf("stgn%d" % i) for i in range(4)]
        grA_t = sb("grA_t", [8, 768]); grA = grA_t[:, :]; grB_ = sc[0:8, 0, 0:384]; grBuf = Buf("gr")
        t5s = sb("t5s", [32, 8]); ohs = sc[0:32, 1, 0:384]; relc = sb("relc", [8, 1])

        pb = [es.enter_context(nc.psum_tensor("pb%d" % i, [128, 512], F32)) for i in range(8)]
        pbB = [Buf("pb%d" % i, psum=True) for i in range(8)]
        g2aB = Buf("g2a"); g2bB = Buf("g2b")

        for nm in ("c0", "c1", "xl0", "xl1", "ys0_0", "ys0_1", "ys0_2", "ys0_3", "ys1_0", "ys1_1", "ys1_2", "ys1_3", "kv0", "kv1", "wd", "ck0", "ck1", "cv0", "cv1", "ckb0", "ckb1", "cvb0", "cvb1", "misc", "wdwb", "wdh", "bias") + tuple("r%d" % i for i in range(7)) + tuple("wb%d" % i for i in range(NSLOT)) + tuple("rh%d" % i for i in range(NSLOT)):
            S.new_sem(nm)

        cnt = {"acc": 0, "misc": 0, "kv": 0, "ys": 0, "ev": 0}

        def acc_bank():
            cnt["acc"] += 1
            return 4 + cnt["acc"] % 2

        def misc_bank():
            cnt["misc"] += 1
            return 6 + cnt["misc"] % 2

        def evac_copy(out_ap, in_ap, reads, writes, eng=None):
            if eng is None:
                cnt["ev"] += 1
                eng = "act" if cnt["ev"] % 2 else "dve"
            if eng == "act":
                S.op("act", lambda e: e.activation(out=out_ap, in_=in_ap, func=AF.Copy), reads=reads, writes=writes)
            else:
                S.op("dve", lambda e: e.tensor_copy(out=out_ap, in_=in_ap), reads=reads, writes=writes)

        def mk3(ap2d, c0, w):
            return ap2d.rearrange("(k p) n -> p k n", p=128)[:, :, c0:c0 + w]

        pieces = []
        NMOD = 36
        MSLOT = 7
        xslotB = [Buf("xs%d" % i) for i in range(3)]

        def slot_of(n):
            return (n % MSLOT) if n < NMOD else ((n - NMOD) % NSLOT)

        def slot_ap(sl):
            if sl < NSLOT:
                return ring[:, sl, :]
            return wd[:, 2 * (sl - NSLOT):2 * (sl - NSLOT) + 2, :].rearrange("p a n -> p (a n)")

        def slot_buf(sl):
            return ringB[sl] if sl < NSLOT else xslotB[sl - NSLOT]

        def add_gu(wg, wu, f):
            for j in range(NJ):
                pieces.append(("gu", wg, wu, j, ("gu", f, j)))

        for pi in range(36):
            pieces.append(("w", w_mod, pi, 256, None))
        for gi in range(6):
            add_gu(w1g, w1u, 1)
            cbs = [4, 5, 8, 2, 3] if gi == 0 else [4, 5, 8, 0, 1, 2, 3, 6, 7]
            for cb in cbs:
                pieces.append(("w", w_in, cb, 256, ("win", cb)))
            if gi > 0:
                for c in range(4):
                    pieces.append(("w", w_out, c, 256, ("wout", c)))
                add_gu(w2g, w2u, 2)
        ws = {"emitted": 0, "next": 0}
        scr_idx = {}
        scrB = {}

        def ws_emit(n):
            p = pieces[n]
            slot = slot_of(n)
            key = p[-1]
            if key is not None and key in scr_idx:
                src = wsc.ap()[scr_idx[key]]
                q, sm = ("sp", "rh%d" % slot)
                S.dma(q, sm, [lambda e, src=src, slot=slot: e.dma_start(out=ring[:, slot, :], in_=src)],
                      reads=[scrB[key]], writes=[ringB[slot]])
                return
            if p[0] == "gu":
                _, wg, wu, j, _k = p
                d0 = ring[:, slot, 0:1024]
                d1 = ring[:, slot, 1024:2048]
                s0 = wg[j]
                s1 = wu[j]
                fns = [lambda e, d0=d0, s0=s0: e.dma_start(out=d0, in_=s0),
                       lambda e, d1=d1, s1=s1: e.dma_start(out=d1, in_=s1)]
            else:
                _, w, c0, wdt, _k = p
                d0 = slot_ap(slot)
                s0 = w[c0]
                fns = [lambda e, d0=d0, s0=s0: e.dma_start(out=d0, in_=s0)]
            S.dma("pool", "r%d" % slot, fns, writes=[slot_buf(slot)])
            if key is not None:
                scr_idx[key] = len(scr_idx)
                scrB[key] = Buf("scr%d" % scr_idx[key])
                dst = wsc.ap()[scr_idx[key]]
                S.dma("sp", "wb%d" % slot, [lambda e, dst=dst, slot=slot: e.dma_start(out=dst, in_=ring[:, slot, :])],
                      reads=[ringB[slot]], writes=[scrB[key]])

        def ws_fill(base):
            lim = min(NMOD, base + MSLOT) if base < NMOD else min(len(pieces), base + NSLOT)
            while ws["emitted"] < lim:
                ws_emit(ws["emitted"])
                ws["emitted"] += 1

        def ws_next(hold_base=None):
            n = ws["next"]
            ws["next"] += 1
            ws_fill(n if hold_base is None else hold_base)
            return slot_of(n)

        c0f = [
            lambda e: e.dma_start(out=ident[:], in_=ident_d[:, :]),
            lambda e: e.dma_start(out=hneg[:], in_=hneg_d[:, :]),
            lambda e: e.dma_start(out=bmfm[:], in_=bmod_fm[:, :]),
            lambda e: e.dma_start(out=gfm[:], in_=gains_fm[:, :]),
            lambda e: e.dma_start(out=esink[:], in_=sinks[0:1, :].partition_broadcast(128)),
            lambda e: e.dma_start(out=xn[:, 0, :], in_=cvec[0:1, :].partition_broadcast(128)),
            lambda e: e.dma_start(out=t5s[:], in_=t5T[:, :]),
            lambda e: e.dma_start(out=ohs, in_=oh_d[:, :]),
            lambda e: e.dma_start(out=grA_t[:, 0:256], in_=relrev[:, 1:257]),
            lambda e: e.dma_start(out=relc[:], in_=relrev[:, 256:257], allow_slow_non_contiguous=True),
        ]
        c1f = [
            lambda e: e.dma_start(out=y_all[:, 0, :], in_=b_mod[0:1, 2048:3072].partition_broadcast(128)),
            lambda e: e.dma_start(out=y_all[:, 1, :], in_=b_mod[0:1, 5120:6144].partition_broadcast(128)),
            lambda e: e.dma_start(out=y_all[:, 2, :], in_=b_mod[0:1, 8192:9216].partition_broadcast(128)),
            lambda e: e.dma_start(out=x_res[:, 0, :], in_=gains[1:2, :].partition_broadcast(128)),
            lambda e: e.dma_start(out=x_res[:, 1, :], in_=gains[3:4, :].partition_broadcast(128)),
            lambda e: e.dma_start(out=x_res[:, 2, :], in_=gains[5:6, :].partition_broadcast(128)),
            lambda e: e.dma_start(out=gg[:], in_=ggain[0:1, :].partition_broadcast(128)),
        ]
        for b in range(4):
            c0f.append(lambda e, b=b: e.dma_start(out=xn[32 * b:32 * b + 32, 1, :],
                                                  in_=cvec[1 + b:2 + b, :].partition_broadcast(32)))
        constB = [identB, smallB, xnB[0], xnB[1], grBuf, scB[0], scB[1]]
        S.dma("sp", "c0", c0f, writes=constB)
        S.dma("act", "c1", c1f, writes=[ggB, yB[0], yB[1], yB[2], xB[0], xB[1], xB[2]])

        ws_fill(0)

        S.op("pool", lambda e: e.memset(VA[:, :, :, 64:65], 1.0), writes=vaB)
        S.op("pool", lambda e: e.memset(VB[:, :, :, 64:65], 1.0), writes=vbB)
        S.op("pool", lambda e: e.memset(VnA[:, :, :, 64:65], 1.0), writes=[vnaB])
        S.op("pool", lambda e: e.memset(VnB[:, :, :, 64:65], 1.0), writes=[vnbB])
        S.op("dve", lambda e: e.tensor_copy(out=identb[:], in_=ident[:]), reads=[identB], writes=[identbB])
        S.op("pool", lambda e: e.memset(eps_t[:], EPS), writes=[smallB])
        S.op("act", lambda e: e.activation(out=esink[:], in_=esink[:], func=AF.Exp), reads=[smallB], writes=[smallB])

        S.op("dve", lambda e: e.tensor_copy(out=grA_t[:, 256:768], in_=relc[:, 0:1].to_broadcast([8, 512])),
             reads=[grBuf], writes=[grBuf])
        S.op("pe", lambda e: e.matmul(pb[6][0:8, 0:384], lhsT=t5s[:, :], rhs=ohs, start=True, stop=True),
             reads=[grBuf, scB[1]], writes=[pbB[6]])
        S.op("dve", lambda e: e.tensor_scalar(out=grA, in0=grA, scalar1=1.0 / SCALE, scalar2=None, op0=ALU.mult),
             reads=[grBuf], writes=[grBuf])
        S.op("dve", lambda e: e.tensor_scalar(out=grB_, in0=pb[6][0:8, 0:384], scalar1=1.0 / SCALE, scalar2=None, op0=ALU.mult),
             reads=[pbB[6]], writes=[grBuf, scB[0]])
        import os
        KCUT = int(os.environ.get("KCUT", "0"))

        def cut(n):
            if KCUT == n:
                S.dead = True
        cut(1)

        def mod_phase(after8):
            for i, (dstT, dB) in enumerate(((SpT, sptB), (SsT, sstB))):
                S.op("act", lambda e, i=i: e.activation(out=xn[:, i, :], in_=xn[:, i, :], func=AF.Silu),
                     reads=[xnB[i]], writes=[xnB[i]])
                for hf in range(2):
                    bk = misc_bank()

                    def tr(e, i=i, hf=hf, bk=bk):
                        for k in range(4):
                            ins = e.transpose(out=pb[bk][:, k * 128:(k + 1) * 128],
                                              in_=xn[:, i, (hf * 4 + k) * 128:(hf * 4 + k + 1) * 128], identity=ident[:])
                        return ins
                    S.op("pe", tr, reads=[xnB[i], identB], writes=[pbB[bk]])
                    evac_copy(dstT[:, hf * 4:hf * 4 + 4, :], pb[bk][:, :].rearrange("p (k t) -> p k t", k=4),
                              [pbB[bk]], [dB])
            S.op("dve", lambda e: e.memset(S5[:], 0.0), writes=[s5B])
            S.op("dve", lambda e: e.tensor_copy(out=S5[:, :, 0:1], in_=SpT[:, :, 0:1]), reads=[sptB], writes=[s5B])
            S.op("dve", lambda e: e.tensor_copy(out=S5[:, :, 1:5], in_=SsT[:, :, 0:128:32]), reads=[sstB], writes=[s5B])
            S.op("dve", lambda e: e.memset(modT[:], 0.0), writes=[modB])

            for pi in range(36):
                if pi == 8:
                    after8()
                slot = ws_next()
                seg = pi // 4
                rs = slot_ap(slot).rearrange("p (k n) -> p k n", k=8)
                rsB = slot_buf(slot)
                if seg % 3 != 2:
                    for cc in range(2):
                        ko = (pi % 4) * 2 + cc
                        bk = misc_bank()

                        def mm(e, rs=rs, cc=cc, bk=bk):
                            for k in range(8):
                                ins = e.matmul(pb[bk][:, 0:5], lhsT=rs[:, k, cc * 128:(cc + 1) * 128], rhs=S5[:, k, 0:5],
                                               start=(k == 0), stop=(k == 7))
                            return ins
                        S.op("pe", mm, reads=[rsB, s5B], writes=[pbB[bk]])
                        S.op("dve", lambda e, seg=seg, ko=ko, bk=bk: e.tensor_scalar(
                            out=modT[:, seg, ko, 0:5], in0=pb[bk][:, 0:5], scalar1=bmfm[:, seg * 8 + ko:seg * 8 + ko + 1],
                            scalar2=None, op0=ALU.add), reads=[pbB[bk], smallB], writes=[modB])
                else:
                    sub = seg // 3
                    c0 = (pi % 4) * 256
                    for which, (ST, stB_, Ct, CB) in enumerate(((SpT, sptB, Cp, CpB), (SsT, sstB, Cs, CsB))):
                        bk = misc_bank()

                        def mm(e, rs=rs, ST=ST, bk=bk):
                            for k in range(8):
                                ins = e.matmul(pb[bk][:, 0:256], lhsT=ST[:, k, :], rhs=rs[:, k, :],
                                               start=(k == 0), stop=(k == 7))
                            return ins
                        S.op("pe", mm, reads=[rsB, stB_], writes=[pbB[bk]])
                        S.op("dve", lambda e, bk=bk, sub=sub, c0=c0: e.tensor_tensor(
                            out=tmp[:, 0:256], in0=pb[bk][:, 0:256], in1=y_all[:, sub, c0:c0 + 256], op=ALU.add),
                            reads=[pbB[bk], yB[sub]], writes=[tmpB])
                        res = 1.0 if sub == 1 else 0.5
                        S.op("dve", lambda e, Ct=Ct, sub=sub, c0=c0, res=res: e.scalar_tensor_tensor(
                            out=Ct[:, sub, c0:c0 + 256], in0=tmp[:, 0:256], scalar=res, in1=x_res[:, sub, c0:c0 + 256],
                            op0=ALU.mult, op1=ALU.mult), reads=[tmpB, xB[sub]], writes=[CB])
            for s in ():
                S.op("dve", lambda e, s=s: e.tensor_scalar(out=Aall[:, s, :, :], in0=modT[:, 3 * s + 1, :, :],
                                                           scalar1=1.0, scalar2=None, op0=ALU.add),
                     reads=[modB], writes=[AB])
                S.op("dve", lambda e, s=s: e.tensor_tensor(
                    out=Aall[:, s, :, :], in0=Aall[:, s, :, :],
                    in1=gfm[:, 16 * s:16 * s + 8].unsqueeze(2).to_broadcast([128, 8, 8]), op=ALU.mult),
                    reads=[AB, smallB], writes=[AB])

            S.dma("sp", "misc", [
                lambda e: e.dma_start(out=g2a.ap()[:, :, :], in_=grA.unsqueeze(1).to_broadcast([8, 128, 768])),
                lambda e: e.dma_start(out=g2b.ap()[:, :, :], in_=grB_.unsqueeze(1).to_broadcast([8, 128, 384])),
            ], reads=[grBuf, scB[0]], writes=[g2aB, g2bB])

        cut(2)

        def Acol(s, k, bc):
            return Aall[:, s, k, bc:bc + 1]

        def Bcol(s, k, bc):
            return modT[:, 3 * s, k, bc:bc + 1]

        def rstd_chain(n, inv_n, col0, SB, np_=128):
            a, c = col0, col0 + 2 * n
            S.op("act", lambda e: e.activation(out=st[0:np_, c:c + n], in_=st[0:np_, a:a + n], func=AF.Sqrt,
                                               scale=inv_n, bias=eps_t[0:np_, 0:1]),
                 reads=[SB, smallB], writes=[SB])
            S.op("dve", lambda e: e.reciprocal(out=st[0:np_, c:c + n], in_=st[0:np_, c:c + n]), reads=[SB], writes=[SB])
            return c

        def sumsq(src_ap, col, reads, SB, np_=128):
            S.op("act", lambda e: e.activation(out=junk[0:np_, 0:src_ap.shape[-1]], in_=src_ap, func=AF.Square,
                                               accum_out=st[0:np_, col:col + 1]),
                 reads=reads + [SB], writes=[junkB, SB])

        class Pipe:
            def __init__(self, stages_fn):
                self.fn = stages_fn
                self.live = []

            def step(self):
                for st_ in self.live:
                    if st_:
                        st_.pop(0)()
                self.live = [x for x in self.live if x]

            def push(self, lt):
                self.step()
                stg = self.fn(lt)
                stg.pop(0)()
                if stg:
                    self.live.append(stg)

            def flush(self):
                while self.live:
                    self.step()

        def norm_stages(ci, s_next, sample, final_row0, lt):
            stages = []
            xt, xb_, xidx = X["t"], X["B"], X["i"]
            if ci is not None:
                Ct, CB = (Cs, CsB) if sample else (Cp, CpB)
                SBp = stpost[lt]
                p0 = 16 + 3 * lt

                def s1():
                    S.op("dve", lambda e: e.memset(st[:, p0:p0 + 3], 0.0), reads=[SBp], writes=[SBp])
                    sumsq(y_all[:, lt, :], p0, [yB[lt]], SBp)
                    S.op("act", lambda e: e.activation(out=st[:, p0 + 2:p0 + 3], in_=st[:, p0:p0 + 1], func=AF.Sqrt,
                                                       scale=1.0 / D, bias=eps_t[:, 0:1]), reads=[SBp, smallB], writes=[SBp])

                def s2():
                    S.op("dve", lambda e: e.reciprocal(out=st[:, p0 + 2:p0 + 3], in_=st[:, p0 + 2:p0 + 3]), reads=[SBp], writes=[SBp])
                    S.op("dve", lambda e: e.scalar_tensor_tensor(
                        out=tmp[:, :], in0=y_all[:, lt, :], scalar=st[:, p0 + 2:p0 + 3], in1=Ct[:, ci, :],
                        op0=ALU.mult, op1=ALU.mult), reads=[yB[lt], SBp, CB], writes=[tmpB])
                    S.op("pool", lambda e: e.tensor_tensor(out=xt[:, lt, :], in0=xt[:, lt, :], in1=tmp[:, :], op=ALU.add),
                         reads=[xb_[lt], tmpB], writes=[xb_[lt]])
                    if final_row0 is not None:
                        r0 = final_row0 + lt * 128
                        S.dma("sp", "ys%d_%d" % (xidx, lt), [lambda e: e.dma_start(out=y_out[r0:r0 + 128, :], in_=xt[:, lt, :])],
                              reads=[xb_[lt]])
                stages += [s1, s2]
            if s_next is not None:
                SBq = stpre[lt]
                q0 = 3 * lt
                xi = lt % 2
                bks = {}

                def s3():
                    S.op("dve", lambda e: e.memset(st[:, q0:q0 + 3], 0.0), reads=[SBq], writes=[SBq])
                    sumsq(xt[:, lt, :], q0, [xb_[lt]], SBq)
                    S.op("act", lambda e: e.activation(out=st[:, q0 + 2:q0 + 3], in_=st[:, q0:q0 + 1], func=AF.Sqrt,
                                                       scale=1.0 / D, bias=eps_t[:, 0:1]), reads=[SBq, smallB], writes=[SBq])

                def s4():
                    S.op("dve", lambda e: e.reciprocal(out=st[:, q0 + 2:q0 + 3], in_=st[:, q0 + 2:q0 + 3]), reads=[SBq], writes=[SBq])
                    S.op("dve", lambda e: e.tensor_scalar(out=xn[:, xi, :], in0=xt[:, lt, :], scalar1=st[:, q0 + 2:q0 + 3],
                                                          scalar2=None, op0=ALU.mult), reads=[xb_[lt], SBq], writes=[xnB[xi]])

                def s5():
                    for hf in range(2):
                        bk = misc_bank()
                        bks[hf] = bk

                        def tr(e, hf=hf, bk=bk):
                            for k in range(4):
                                ins = e.transpose(out=pb[bk][:, k * 128:(k + 1) * 128],
                                                  in_=xn[:, xi, (hf * 4 + k) * 128:(hf * 4 + k + 1) * 128], identity=ident[:])
                            return ins
                        S.op("pe", tr, reads=[xnB[xi], identB], writes=[pbB[bk]])

                def s6():
                    for hf in range(2):
                        bk = bks[hf]
                        for kk in range(4):
                            k = hf * 4 + kk
                            blocks = [(0, 128, 0)] if not sample else [(32 * b, 32, 1 + b) for b in range(4)]
                            for (o0, w, bc) in blocks:
                                oap = hT[:, k, lt * 128 + o0:lt * 128 + o0 + w]
                                iap = pb[bk][:, kk * 128 + o0:kk * 128 + o0 + w]
                                if hf == 0:
                                    S.op("act", lambda e, oap=oap, iap=iap, k=k, bc=bc: e.activation(
                                        out=oap, in_=iap, func=AF.Identity, scale=Acol(s_next, k, bc), bias=Bcol(s_next, k, bc)),
                                        reads=[pbB[bk], AB, modB], writes=[hBt[k][lt]])
                                else:
                                    S.op("dve", lambda e, oap=oap, iap=iap, k=k, bc=bc: e.tensor_scalar(
                                        out=oap, in0=iap, scalar1=Acol(s_next, k, bc), scalar2=Bcol(s_next, k, bc),
                                        op0=ALU.mult, op1=ALU.add), reads=[pbB[bk], AB, modB], writes=[hBt[k][lt]])
                stages += [s3, s4, s5, s6]
            return stages

        def prenorm(s, nt, sample):
            P_ = Pipe(lambda lt: norm_stages(None, s, sample, None, lt))
            for lt in range(nt):
                P_.push(lt)
            P_.flush()

        wd_seen = {}

        def ffn(wdn, nt, f, after_tile=None, before_last=None, extra=None):
            T = nt * 128
            j0 = 0
            for sup, js in enumerate(SUPS):
                for jl in range(js):
                    j = j0 + jl
                    slot = ws_next()
                    gb, ub = j % 2, 2 + j % 2
                    rg = ring[:, slot, 0:1024].rearrange("p (k n) -> p k n", k=8)
                    ru = ring[:, slot, 1024:2048].rearrange("p (k n) -> p k n", k=8)

                    def mmg(e, rr=rg, bk=gb):
                        for k in range(8):
                            ins = e.matmul(pb[bk][:, 0:T], lhsT=rr[:, k, :], rhs=hT[:, k, 0:T], start=(k == 0), stop=(k == 7))
                        return ins

                    def mmu(e, rr=ru, bk=ub):
                        for k in range(8):
                            ins = e.matmul(pb[bk][:, 0:T], lhsT=rr[:, k, :], rhs=hT[:, k, 0:T], start=(k == 0), stop=(k == 7))
                        return ins
                    S.op("pe", mmg, reads=[ringB[slot]] + hB, writes=[pbB[gb]])
                    S.op("pe", mmu, reads=[ringB[slot]] + hB, writes=[pbB[ub]])
                    si = j % 2
                    S.op("act", lambda e, si=si, gb=gb: e.activation(out=sg[:, si, 0:T], in_=pb[gb][:, 0:T], func=AF.Silu),
                         reads=[pbB[gb]], writes=[sgB[si]])
                    S.op("dve", lambda e, si=si, ub=ub, jl=jl: e.tensor_tensor(out=aT[:, jl, 0:T], in0=sg[:, si, 0:T],
                                                                               in1=pb[ub][:, 0:T], op=ALU.mult),
                         reads=[sgB[si], pbB[ub]], writes=[aB[jl]])
                    if jl == 1:
                        sview = wdsc.ap()[f - 1].rearrange("p (j n) -> p j n", j=NJ)[:, j0:j0 + js, :]
                        if (f, sup) in wd_seen:
                            S.dma("sp", "wdh", [lambda e, sview=sview, js=js: e.dma_start(out=wd[:, 0:js, :], in_=sview)],
                                  reads=[wd_seen[(f, sup)]], writes=[wdB] + ckBs + ckbBs + xslotB)
                        else:
                            src = wdn[:, j0:j0 + js, :]
                            S.dma("pool", "wd", [lambda e, src=src, js=js: e.dma_start(out=wd[:, 0:js, :], in_=src)], writes=[wdB] + ckBs + ckbBs + xslotB)
                            wd_seen[(f, sup)] = Buf("wdsc%d%d" % (f, sup))
                            S.dma("sp", "wdwb", [lambda e, sview=sview, js=js: e.dma_start(out=sview, in_=wd[:, 0:js, :])],
                                  reads=[wdB], writes=[wd_seen[(f, sup)]])
                if before_last is not None and sup == len(SUPS) - 1:
                    before_last()
                for lt in range(nt):
                    for half in range(2):
                        bk = acc_bank()

                        def mmd(e, lt=lt, half=half, bk=bk, js=js):
                            for jl in range(js):
                                ins = e.matmul(pb[bk][:, 0:512], lhsT=aT[:, jl, lt * 128:(lt + 1) * 128],
                                               rhs=wd[:, jl, half * 512:(half + 1) * 512], start=(jl == 0), stop=(jl == js - 1))
                            return ins
                        S.op("pe", mmd, reads=aB[0:js] + [wdB], writes=[pbB[bk]])
                        yap = y_all[:, lt, half * 512:(half + 1) * 512]
                        if sup == 0:
                            evac_copy(yap, pb[bk][:, 0:512], [pbB[bk]], [yB[lt]])
                        else:
                            S.op("dve", lambda e, yap=yap, bk=bk: e.tensor_tensor(out=yap, in0=yap, in1=pb[bk][:, 0:512],
                                                                                  op=ALU.add),
                                 reads=[pbB[bk], yB[lt]], writes=[yB[lt]])
                    if extra is not None and sup == len(SUPS) - 1 and lt < extra[1]:
                        extra[0].push(lt)
                    if after_tile is not None and sup == len(SUPS) - 1 and lt >= 1:
                        after_tile.push(lt - 1)
                if extra is not None and sup == len(SUPS) - 1:
                    for lt2 in range(nt, extra[1]):
                        extra[0].push(lt2)
                    extra[0].flush()
                if after_tile is not None and sup == len(SUPS) - 1:
                    after_tile.push(nt - 1)
                    after_tile.flush()
                j0 += js

        def proj(kind, ti0, nt):
            T = nt * 128
            sample = kind == "sample"
            cbs = [4, 5, 8, 2, 3] if kind == "halo" else [4, 5, 8, 0, 1, 2, 3, 6, 7]
            kcol0 = (ti0 % 8) * 128

            def fm_group(rs, cc, dst_ap, wbufs):
                bk = acc_bank()

                def mm(e, bk=bk):
                    for k in range(8):
                        ins = e.matmul(pb[bk][:, 0:T], lhsT=rs[:, k, cc * 128:(cc + 1) * 128], rhs=hT[:, k, 0:T],
                                       start=(k == 0), stop=(k == 7))
                    return ins
                S.op("pe", mm, reads=[ringB[slot]] + hB, writes=[pbB[bk]])
                evac_copy(dst_ap, pb[bk][:, 0:T], [pbB[bk]], wbufs)

            def tm_group(rs, c0, w, tcol0, m):
                bk = acc_bank()

                def mm(e, bk=bk):
                    for k in range(8):
                        ins = e.matmul(pb[bk][0:m, 0:w], lhsT=hT[:, k, tcol0:tcol0 + m], rhs=rs[:, k, c0:c0 + w],
                                       start=(k == 0), stop=(k == 7))
                    return ins
                S.op("pe", mm, reads=[ringB[slot]] + [hBt[k][tcol0 // 128] for k in range(8)], writes=[pbB[bk]])
                return bk

            for cb in cbs:
                slot = ws_next()
                rs = ring[:, slot, :].rearrange("p (k n) -> p k n", k=8)
                if cb in (0, 1):
                    for cc in range(2):
                        c = (cb % 2) * 2 + cc
                        fm_group(rs, cc, QAT[:, c, 0:T], [qaB[c]])
                elif cb in (6, 7):
                    for cc in range(2):
                        c = (cb % 2) * 2 + cc
                        fm_group(rs, cc, QBT[:, c, 0:T], [qbB[c]])
                elif cb in (2, 3):
                    for cc in range(2):
                        c = (cb % 2) * 2 + cc
                        if sample:
                            fm_group(rs, cc, KnA[:, c, 0:T], [knaB])
                        else:
                            fm_group(rs, cc, KAT[:, c, kcol0:kcol0 + T], [kaB[(ti0 + i) % 8] for i in range(nt)])
                    if kind == "last":
                        for lt in range(nt):
                            bk = tm_group(rs, 0, 256, lt * 128, 128)
                            kv_out_b(bk, 256, kap[lt * 128:(lt + 1) * 128, (cb % 2) * 256:(cb % 2) * 256 + 256])
                    if sample:
                        bk = tm_group(rs, 0, 256, 0, 128)
                        kv_out_b(bk, 256, kas[:, (cb % 2) * 256:(cb % 2) * 256 + 256])
                elif cb in (4, 5):
                    h0 = (cb % 2) * 4
                    if not sample:
                        for lt in range(nt):
                            bk = tm_group(rs, 0, 256, lt * 128, 128)
                            sl = (ti0 + lt) % 8
                            evac_copy(VA[:, sl, h0:h0 + 4, 0:64], pb[bk][:, 0:256].rearrange("p (h d) -> p h d", h=4),
                                      [pbB[bk]], [vaB[sl]])
                            if kind == "last":
                                kv_out_b(bk, 256, vap[lt * 128:(lt + 1) * 128, h0 * 64:h0 * 64 + 256])
                    else:
                        for b in range(4):
                            bk = tm_group(rs, 0, 256, 32 * b, 32)
                            evac_copy(VnA[0:32, b, h0:h0 + 4, 0:64], pb[bk][0:32, 0:256].rearrange("p (h d) -> p h d", h=4),
                                      [pbB[bk]], [vnaB])
                        bk = tm_group(rs, 0, 256, 0, 128)
                        kv_out_b(bk, 256, vas[:, h0 * 64:h0 * 64 + 256])
                else:
                    if sample:
                        fm_group(rs, 0, KnB[:, 0:T], [knbB])
                        for b in range(4):
                            bk = tm_group(rs, 0, 256, 32 * b, 32)
                            evac_copy(VnB[0:32, b, :, 0:64], pb[bk][0:32, 128:256].rearrange("p (h d) -> p h d", h=2),
                                      [pbB[bk]], [vnbB])
                        bk = tm_group(rs, 0, 256, 0, 128)
                        kv_out_b(bk, 128, kbs[:, :], 128, 0)
                        kv_out_b(bk, 128, vbs[:, :], 128, 128)
                    else:
                        for lt in range(nt):
                            bk = tm_group(rs, 0, 256, lt * 128, 128)
                            sl = (ti0 + lt) % 8
                            evac_copy(VB[:, sl, :, 0:64], pb[bk][:, 128:256].rearrange("p (h d) -> p h d", h=2),
                                      [pbB[bk]], [vbB[sl]])
                            if kind == "last" and lt == nt - 1:
                                kv_out_b(bk, 128, kbp[:, :], 128, 0)
                                kv_out_b(bk, 128, vbp[:, :], 128, 128)
                        fm_group(rs, 0, KBT[:, kcol0:kcol0 + T], [kbB[(ti0 + i) % 8] for i in range(nt)])

        def kv_out_b(bk, w, dst_ap, np_=128, c0=0):
            cnt["kv"] += 1
            ki = cnt["kv"] % 2
            S.op("dve", lambda e: e.tensor_copy(out=kvst[0:np_, ki, 0:w], in_=pb[bk][0:np_, c0:c0 + w]),
                 reads=[pbB[bk]], writes=[kvB[ki]])
            S.dma("sp", "kv%d" % ki, [lambda e: e.dma_start(out=dst_ap, in_=kvst[0:np_, ki, 0:w])], reads=[kvB[ki]])

        def attention(nq, qcol0, o_ap, oB, keyA, keyB):
            W = 4 * nq
            for i, (np_, kf, kbuf, vf, vbuf, kt, mask) in enumerate(keyA):
                X, Y = i % 2, 2 + i % 2

                def mm(e, np_=np_, kf=kf, X=X, Y=Y, kt=kt):
                    e.matmul(pb[X][0:np_, 0:W], lhsT=identb[0:np_, 0:np_], rhs=biasA[0:np_, kt, 0, :, 0:nq],
                             start=True, stop=False)
                    e.matmul(pb[Y][0:np_, 0:W], lhsT=identb[0:np_, 0:np_], rhs=biasA[0:np_, kt, 1, :, 0:nq],
                             start=True, stop=False)
                    for jh in range(4):
                        e.matmul(pb[X][0:np_, jh * nq:(jh + 1) * nq], lhsT=kf(0, jh), rhs=QAT[0:64, jh, qcol0:qcol0 + nq],
                                 start=False, stop=(jh == 3))
                        ins = e.matmul(pb[Y][0:np_, jh * nq:(jh + 1) * nq], lhsT=kf(1, jh),
                                       rhs=QAT[64:128, jh, qcol0:qcol0 + nq], start=False, stop=(jh == 3))
                    return ins
                S.op("pe", mm, reads=[kbuf, bAB, identbB] + qaB, writes=[pbB[X], pbB[Y]])
                for par, bnk in ((0, X), (1, Y)):
                    if mask:
                        S.op("act", lambda e, np_=np_, par=par, i=i, bnk=bnk: e.activation(
                            out=PT[0:np_, par, i, 0:W], in_=pb[bnk][0:np_, 0:W], func=AF.Exp, scale=SCALE,
                            bias=hneg[0:np_, 0:1]), reads=[pbB[bnk], smallB], writes=[ptB[par][i]])
                    else:
                        S.op("act", lambda e, np_=np_, par=par, i=i, bnk=bnk: e.activation(
                            out=PT[0:np_, par, i, 0:W], in_=pb[bnk][0:np_, 0:W], func=AF.Exp, scale=SCALE),
                            reads=[pbB[bnk]], writes=[ptB[par][i]])
            for i, (np_, kf, kbuf, vf, vbuf, kt) in enumerate(keyB):
                X, Y = (i + 1) % 2, 2 + (i + 1) % 2

                def mm(e, np_=np_, kf=kf, X=X, Y=Y, kt=kt):
                    e.matmul(pb[X][0:np_, 0:W], lhsT=identb[0:np_, 0:np_], rhs=biasB[0:np_, kt, 0:4, 0:nq],
                             start=True, stop=False)
                    e.matmul(pb[Y][0:np_, 0:W], lhsT=identb[0:np_, 0:np_], rhs=biasB[0:np_, kt, 4:8, 0:nq],
                             start=True, stop=False)
                    e.matmul(pb[X][0:np_, 0:W], lhsT=kf(0), rhs=QBT[0:64, :, qcol0:qcol0 + nq], start=False, stop=True)
                    return e.matmul(pb[Y][0:np_, 0:W], lhsT=kf(1), rhs=QBT[64:128, :, qcol0:qcol0 + nq],
                                    start=False, stop=True)
                S.op("pe", mm, reads=[kbuf, bBB, identbB] + qbB, writes=[pbB[X], pbB[Y]])
                for c, bnk in ((0, X), (1, Y)):
                    mask = (i == 0 and len(keyA) > 0 and keyA[3][6]) if nq == 128 else False
                    if mask:
                        S.op("act", lambda e, np_=np_, c=c, i=i, bnk=bnk: e.activation(
                            out=PTB[0:np_, c, i, 0:W], in_=pb[bnk][0:np_, 0:W], func=AF.Exp, scale=SCALE,
                            bias=hneg[0:np_, 0:1]), reads=[pbB[bnk], smallB], writes=[ptbB[c][i]])
                    else:
                        S.op("act", lambda e, np_=np_, c=c, i=i, bnk=bnk: e.activation(
                            out=PTB[0:np_, c, i, 0:W], in_=pb[bnk][0:np_, 0:W], func=AF.Exp, scale=SCALE),
                            reads=[pbB[bnk]], writes=[ptbB[c][i]])
            for par in range(2):
                ob = 4 + par

                def pv(e, par=par, ob=ob):
                    for jh in range(4):
                        h = 2 * jh + par
                        for i, (np_, kf, kbuf, vf, vbuf, kt, mask) in enumerate(keyA):
                            ins = e.matmul(pb[ob][0:nq, jh * 65:(jh + 1) * 65], lhsT=PT[0:np_, par, i, jh * nq:(jh + 1) * nq],
                                           rhs=vf(h), start=(i == 0), stop=(i == len(keyA) - 1))
                    return ins
                S.op("pe", pv, reads=ptB[par] + [k[4] for k in keyA], writes=[pbB[ob]])
            for c in range(2):
                ob = 6 + c

                def pvb(e, c=c, ob=ob):
                    for hh in range(4):
                        for i, (np_, kf, kbuf, vf, vbuf, kt) in enumerate(keyB):
                            ins = e.matmul(pb[ob][0:nq, hh * 65:(hh + 1) * 65], lhsT=PTB[0:np_, c, i, hh * nq:(hh + 1) * nq],
                                           rhs=vf(c), start=(i == 0), stop=(i == len(keyB) - 1))
                    return ins
                S.op("pe", pvb, reads=ptbB[c] + [k[4] for k in keyB], writes=[pbB[ob]])
            for par in range(2):
                ob = 4 + par
                Ov = pb[ob][0:nq, 0:260].rearrange("p (h e) -> p h e", e=65)
                S.op("dve", lambda e, Ov=Ov, par=par: e.reciprocal(out=st[0:nq, 32 + 4 * par:36 + 4 * par].unsqueeze(2),
                                                                   in_=Ov[:, :, 64:65]),
                     reads=[pbB[ob], stB], writes=[stB])
                S.op("dve", lambda e, Ov=Ov, par=par: e.tensor_tensor(
                    out=o_ap[:, 0:512].rearrange("p (j r d) -> p j r d", j=4, r=2)[:, :, par, :], in0=Ov[:, :, 0:64],
                    in1=st[0:nq, 32 + 4 * par:36 + 4 * par].unsqueeze(2).to_broadcast([nq, 4, 64]), op=ALU.mult),
                    reads=[pbB[ob], stB], writes=[oB])
            for c in range(2):
                ob = 6 + c
                Ov = pb[ob][0:nq, 0:260].rearrange("p (h e) -> p h e", e=65)
                S.op("dve", lambda e, Ov=Ov, c=c: e.tensor_tensor(out=st[0:nq, 40 + 4 * c:44 + 4 * c].unsqueeze(2),
                                                                  in0=Ov[:, :, 64:65],
                                                                  in1=esink[0:nq, 4 * c:4 * c + 4].unsqueeze(2), op=ALU.add),
                     reads=[pbB[ob], stB, smallB], writes=[stB])
                S.op("dve", lambda e, c=c: e.reciprocal(out=st[0:nq, 40 + 4 * c:44 + 4 * c],
                                                        in_=st[0:nq, 40 + 4 * c:44 + 4 * c]), reads=[stB], writes=[stB])
                S.op("dve", lambda e, Ov=Ov, c=c: e.tensor_tensor(
                    out=o_ap[:, 512 + 256 * c:768 + 256 * c].rearrange("p (h d) -> p h d", h=4), in0=Ov[:, :, 0:64],
                    in1=st[0:nq, 40 + 4 * c:44 + 4 * c].unsqueeze(2).to_broadcast([nq, 4, 64]), op=ALU.mult),
                    reads=[pbB[ob], stB], writes=[oB])

        def groupnorm_stats(o_ap, oB, nq, gi):
            SB = stgn[gi]
            col = 48 + 6 * gi
            S.op("dve", lambda e: e.memset(st[0:nq, col:col + 2], 0.0), reads=[SB], writes=[SB])
            sumsq(o_ap[:, 0:512], col, [oB], SB, nq)
            sumsq(o_ap[:, 512:1024], col + 1, [oB], SB, nq)
            return rstd_chain(2, 1.0 / 512, col, SB, nq)

        def groupnorm_apply(o_ap, oB, nq, tcol0, gi, c):
            for f_ in groupnorm_stages(o_ap, oB, nq, tcol0, gi, c):
                f_()

        def groupnorm_stages(o_ap, oB, nq, tcol0, gi, c):
            SB = stgn[gi]
            xi = gi % 2
            bks = {}

            def g1():
                for hf in range(2):
                    S.op("dve", lambda e, hf=hf: e.scalar_tensor_tensor(
                        out=xn[0:nq, xi, hf * 512:(hf + 1) * 512], in0=o_ap[:, hf * 512:(hf + 1) * 512],
                        scalar=st[0:nq, c + hf:c + hf + 1], in1=gg[0:nq, hf * 512:(hf + 1) * 512], op0=ALU.mult, op1=ALU.mult),
                        reads=[oB, SB, ggB], writes=[xnB[xi]])

            def g2():
                for hf in range(2):
                    bk = hf * 2 + (cnt["misc"] % 2)
                    cnt["misc"] += 1
                    bks[hf] = bk

                    def tr(e, hf=hf, bk=bk):
                        for k in range(4):
                            ins = e.transpose(out=pb[bk][:, k * nq:(k + 1) * nq],
                                              in_=xn[0:nq, xi, (hf * 4 + k) * 128:(hf * 4 + k + 1) * 128],
                                              identity=ident[0:nq, 0:nq])
                        return ins
                    S.op("pe", tr, reads=[xnB[xi], identB], writes=[pbB[bk]])

            def g3():
                for hf in range(2):
                    bk = bks[hf]
                    for kk in range(4):
                        k = hf * 4 + kk
                        evac_copy(hT[:, k, tcol0:tcol0 + nq], pb[bk][:, kk * nq:(kk + 1) * nq], [pbB[bk]], [hBt[k][tcol0 // 128]],
                                  eng=("act" if hf == 0 else "dve"))
            return [g1, g2, g3]

        def _groupnorm_apply_old(o_ap, oB, nq, tcol0, gi, c):
            SB = stgn[gi]
            xi = gi % 2
            for hf in range(2):
                S.op("dve", lambda e, hf=hf: e.scalar_tensor_tensor(
                    out=xn[0:nq, xi, hf * 512:(hf + 1) * 512], in0=o_ap[:, hf * 512:(hf + 1) * 512],
                    scalar=st[0:nq, c + hf:c + hf + 1], in1=gg[0:nq, hf * 512:(hf + 1) * 512], op0=ALU.mult, op1=ALU.mult),
                    reads=[oB, SB, ggB], writes=[xnB[xi]])
            for hf in range(2):
                bk = hf * 2 + (cnt["misc"] % 2)
                cnt["misc"] += 1

                def tr(e, hf=hf, bk=bk):
                    for k in range(4):
                        ins = e.transpose(out=pb[bk][:, k * nq:(k + 1) * nq],
                                          in_=xn[0:nq, xi, (hf * 4 + k) * 128:(hf * 4 + k + 1) * 128],
                                          identity=ident[0:nq, 0:nq])
                    return ins
                S.op("pe", tr, reads=[xnB[xi], identB], writes=[pbB[bk]])
                for kk in range(4):
                    k = hf * 4 + kk
                    evac_copy(hT[:, k, tcol0:tcol0 + nq], pb[bk][:, kk * nq:(kk + 1) * nq], [pbB[bk]], [hBt[k][tcol0 // 128]],
                              eng=("act" if hf == 0 else "dve"))

        def groupnorm_T(o_ap, oB, nq, tcol0, gi):
            c = groupnorm_stats(o_ap, oB, nq, gi)
            groupnorm_apply(o_ap, oB, nq, tcol0, gi, c)

        def wout_stage(lt, slots):
            for c in range(4):
                slot = slots[c]
                rs = ring[:, slot, :].rearrange("p (k n) -> p k n", k=8)
                bk = acc_bank()

                def mm(e, bk=bk, rs=rs):
                    for k in range(8):
                        ins = e.matmul(pb[bk][:, 0:256], lhsT=hT[:, k, lt * 128:(lt + 1) * 128], rhs=rs[:, k, :],
                                       start=(k == 0), stop=(k == 7))
                    return ins
                S.op("pe", mm, reads=[ringB[slot]] + [hBt[k][lt] for k in range(8)], writes=[pbB[bk]])
                evac_copy(y_all[:, lt, c * 256:(c + 1) * 256], pb[bk][:, 0:256], [pbB[bk]], [yB[lt]])

        def wout_proj(nt, sample, pre=None):
            base = ws["next"]
            slots = [ws_next(hold_base=base) for _ in range(4)]

            def stages(lt):
                stg = list(pre(lt)) if pre is not None else []
                stg.append(lambda: wout_stage(lt, slots))
                return stg + norm_stages(1, 2, sample, None, lt)
            P_ = Pipe(stages)
            for lt in range(nt):
                P_.push(lt)
            P_.flush()
            ws_fill(ws["next"] - 1)

        def load_x(row0, nt, bi):
            src = xin[row0:row0 + nt * 128, :].rearrange("(t p) d -> p t d", p=128)
            S.dma("sp", "xl%d" % bi, [lambda e: e.dma_start(out=xr[bi][:, 0:nt, :], in_=src)], writes=xBs[bi][0:nt])

        def use_x(bi):
            X["t"], X["B"], X["i"] = xr[bi], xBs[bi], bi

        def prompt_keys(ti):
            keyA = []
            for kt in range(5):
                tk = ti - 4 + kt
                sl = tk % 8
                keyA.append((128,
                             (lambda half, c, sl=sl: KAT[64 * half:64 * half + 64, c, sl * 128:(sl + 1) * 128]),
                             kaB[sl], (lambda h, sl=sl: VA[:, sl, h, :]), vaB[sl], kt, tk < 4))
            keyB = []
            for kt in range(2):
                tk = ti - 1 + kt
                sl = tk % 8
                keyB.append((128, (lambda c, sl=sl: KBT[64 * c:64 * c + 64, sl * 128:(sl + 1) * 128]), kbB[sl],
                             (lambda c, sl=sl: VB[:, sl, c, :]), vbB[sl], kt))
            return keyA, keyB

        def chain(ci, s_next, sample, final_row0=None):
            return Pipe(lambda lt: norm_stages(ci, s_next, sample, final_row0, lt))

        def finalize_ab(slist):
            for s_ in slist:
                S.op("dve", lambda e, s_=s_: e.tensor_scalar(out=Aall[:, s_, :, :], in0=modT[:, 3 * s_ + 1, :, :],
                                                             scalar1=1.0, scalar2=None, op0=ALU.add),
                     reads=[modB], writes=[AB])
                S.op("dve", lambda e, s_=s_: e.tensor_tensor(
                    out=Aall[:, s_, :, :], in0=Aall[:, s_, :, :],
                    in1=gfm[:, 16 * s_:16 * s_ + 8].unsqueeze(2).to_broadcast([128, 8, 8]), op=ALU.mult),
                    reads=[AB, smallB], writes=[AB])

        def halo_start():
            finalize_ab([0])
            load_x(0, 4, 1)
            use_x(1)
            prenorm(0, 4, False)

        mod_phase(halo_start)
        finalize_ab([1, 2])
        use_x(1)
        cut(3)
        load_x(4 * 128, 4, 0)
        ffn(w1d, 4, 1, after_tile=chain(0, 1, False))
        cut(4)
        tfn = []
        for kt in range(5):
            for par in range(2):
                src = bass.AP(g2a, 639 - 128 * kt + par * 98304, [[767, 128], [2 * 98304, 4], [1, 128]])
                tfn.append(lambda e, src=src, kt=kt, par=par: e.dma_start(out=biasA[:, kt, par, :, :], in_=src))
        for kt in range(2):
            src = bass.AP(g2b, 255 - 128 * kt, [[383, 128], [49152, 8], [1, 128]])
            tfn.append(lambda e, src=src, kt=kt: e.dma_start(out=biasB[:, kt, :, :], in_=src))
        S.dma("pool", "bias", tfn, reads=[g2aB, g2bB], writes=[bAB, bBB])
        proj("halo", 0, 4)
        cut(5)
        use_x(0)
        prenorm(0, 4, False)
        for g in range(4):
            ti0 = 4 + 4 * g
            bi = g % 2
            use_x(bi)
            ffn(w1d, 4, 1, after_tile=chain(0, 1, False))
            if g < 3:
                load_x((ti0 + 4) * 128, 4, 1 - bi)
            else:
                load_x(2560, 1, 1 - bi)
            proj("last" if g == 3 else "main", ti0, 4)
            cut(6)
            if g == 0:
                S.op("dve", lambda e: e.memset(biasA[0:64, 0, :, :, 64:128], NEG), writes=[bAB])
                S.op("dve", lambda e: e.memset(biasA[64:128, 4, :, :, 0:64], NEG), writes=[bAB])
                S.op("dve", lambda e: e.memset(biasB[0:64, 0, :, 64:128], NEG), writes=[bBB])
                S.op("dve", lambda e: e.memset(biasB[64:128, 1, :, 0:64], NEG), writes=[bBB])
            for lt in range(4):
                keyA, keyB = prompt_keys(ti0 + lt)
                attention(128, lt * 128, y_all[:, lt, :], yB[lt], keyA, keyB)
            cut(7)
            cut(8)

            def gn_pre(lt):
                return ([lambda lt=lt: groupnorm_stats(y_all[:, lt, :], yB[lt], 128, lt)]
                        + groupnorm_stages(y_all[:, lt, :], yB[lt], 128, lt * 128, lt, 48 + 6 * lt + 4))
            wout_proj(4, False, pre=gn_pre)
            cut(9)

            def next_stages(lt, g=g, bi=bi):
                use_x(1 - bi)
                stg = norm_stages(None, 0, g == 3, None, lt)
                use_x(bi)
                return stg
            ffn(w2d, 4, 2, after_tile=chain(2, None, False, final_row0=g * 512),
                extra=(Pipe(next_stages), 4 if g < 3 else 1))
            cut(10 + g)

        use_x(0)
        ffn(w1d, 1, 1, after_tile=chain(0, 1, True))
        proj("sample", 0, 1)
        cut(14)
        for b in range(4):
            bb = b % 2
            ck, ckB, ckb, ckbB = cks[bb], ckBs[bb], ckbs[bb], ckbBs[bb]
            S.dma("pool", "ck%d" % bb, [lambda e, b=b, ck=ck: e.dma_start(out=ck,
                                                                        in_=cak[b].rearrange("(t p) f -> p t f", p=128))],
                  writes=[ckB, wdB])
            for c in range(4):
                bk = c % 4

                def tr(e, c=c, bk=bk, ck=ck):
                    pbv = pb[bk][:, :].bitcast(BF16)
                    for kt in range(4):
                        ins = e.transpose(out=pbv[:, kt * 128:(kt + 1) * 128], in_=ck[:, kt, c * 128:(c + 1) * 128],
                                          identity=identb[:])
                    return ins
                S.op("pe", tr, reads=[ckB, identbB], writes=[pbB[bk]])
                evac_copy(KAT[:, c, bb * 512:(bb + 1) * 512], pb[bk][:, :].bitcast(BF16)[:, 0:512], [pbB[bk]],
                          [kaB[bb * 4 + i] for i in range(4)])
            S.dma("pool", "cv%d" % bb, [lambda e, b=b, bb=bb, kt=kt: e.dma_start(
                out=VA[:, bb * 4 + kt, :, 0:64], in_=cav[b][kt * 128:(kt + 1) * 128, :].rearrange("p (h d) -> p h d", h=8))
                for kt in range(4)], writes=[vaB[bb * 4 + i] for i in range(4)])
            S.dma("pool", "ckb%d" % bb, [lambda e, b=b, ckb=ckb: e.dma_start(out=ckb, in_=cbk[b])], writes=[ckbB, wdB])

            def trb(e, ckb=ckb):
                pbv = pb[0][:, :].bitcast(BF16)
                return e.transpose(out=pbv[:, 0:128], in_=ckb, identity=identb[:])
            S.op("pe", trb, reads=[ckbB, identbB], writes=[pbB[0]])
            evac_copy(KBT[:, bb * 128:(bb + 1) * 128], pb[0][:, :].bitcast(BF16)[:, 0:128], [pbB[0]], [kbB[bb]])
            S.dma("pool", "cvb%d" % bb, [lambda e, b=b, bb=bb: e.dma_start(
                out=VB[:, bb, :, 0:64], in_=cbv[b].rearrange("p (h d) -> p h d", h=2))], writes=[vbB[bb]])
            keyA = []
            for kt in range(4):
                sl = bb * 4 + kt
                keyA.append((128,
                             (lambda half, c, sl=sl: KAT[64 * half:64 * half + 64, c, sl * 128:(sl + 1) * 128]),
                             kaB[sl], (lambda h, sl=sl: VA[:, sl, h, :]), vaB[sl], kt, False))
            keyA.append((32, (lambda half, c, b=b: KnA[64 * half:64 * half + 64, c, 32 * b:32 * b + 32]), knaB,
                         (lambda h, b=b: VnA[0:32, b, h, :]), vnaB, 4, False))
            keyB = [(128, (lambda c, bb=bb: KBT[64 * c:64 * c + 64, bb * 128:(bb + 1) * 128]), kbB[bb],
                     (lambda c, bb=bb: VB[:, bb, c, :]), vbB[bb], 0),
                    (32, (lambda c, b=b: KnB[64 * c:64 * c + 64, 32 * b:32 * b + 32]), knbB,
                     (lambda c, b=b: VnB[0:32, b, c, :]), vnbB, 1)]
            attention(32, 32 * b, y_all[0:32, b, :], yB[b], keyA, keyB)
        gcs = [groupnorm_stats(y_all[0:32, b, :], yB[b], 32, b) for b in range(4)]
        for b in range(4):
            groupnorm_apply(y_all[0:32, b, :], yB[b], 32, 32 * b, b, gcs[b])
        cut(15)
        wout_proj(1, True)
        ffn(w2d, 1, 2, after_tile=chain(2, None, True, final_row0=2048))

        assert S.dead or ws["next"] == len(pieces), (ws["next"], len(pieces))
        S.finish("sp")
        import os
        if os.environ.get("KSTATS"):
            print({e: len(v) for e, v in S.streams.items()}, S.count)
        S.build()
    return nc


def _t5_onehot():
    oh = np.zeros((32, 384), np.float32)
    for m in range(383):
        rel = 127 - m
        n = abs(rel)
        base = 16 if rel > 0 else 0
        if n < 8:
            bk = n
        else:
            bk = min(8 + ((n * n) // 64).bit_length() - 1, 15)
        oh[base + bk, m] = 1.0
    return oh


def _prep(x_prompt, x_sample, cache_a_k, cache_a_v, cache_b_k, cache_b_v, c_prompt, c_sample,
          w_mod, b_mod, norm_gains, w1_gate, w1_up, w1_down, w_in, w_out, group_gains,
          rel_bias_a, t5_bias_table, sinks_b, w2_gate, w2_up, w2_down):
    f = lambda a: np.ascontiguousarray(np.asarray(a, dtype=np.float32))
    x_prompt = f(x_prompt); x_sample = f(x_sample)
    perm = list(range(1536))
    for cc in range(4):
        perm += list(range(1536 + 64 * cc, 1536 + 64 * cc + 64))
        perm += list(range(1536 + 64 * (4 + cc), 1536 + 64 * (4 + cc) + 64))
    perm += list(range(2048, 2304))
    w_in_p = f(np.asarray(w_in)[0][:, perm])
    g6 = f(norm_gains)[0]
    def tile_cols(w, cw):
        n = w.shape[1] // cw
        return f(w.reshape(8, 128, n, cw).transpose(2, 1, 0, 3).reshape(n, 128, 8 * cw))

    def tile_rows(w):
        return f(w.reshape(NJ, 128, D).transpose(1, 0, 2))
    shared = {
        "ident": np.eye(128, dtype=np.float32),
        "w_mod": tile_cols(f(w_mod)[0], 256), "b_mod": f(b_mod)[0:1],
        "bmod_fm": f(f(b_mod)[0].reshape(72, 128).T),
        "gains": g6, "gains_fm": f(g6.reshape(6, 8, 128).transpose(2, 0, 1).reshape(128, 48)),
        "w1_gate": tile_cols(f(w1_gate)[0], 128), "w1_up": tile_cols(f(w1_up)[0], 128), "w1_down": tile_rows(f(w1_down)[0]),
        "w2_gate": tile_cols(f(w2_gate)[0], 128), "w2_up": tile_cols(f(w2_up)[0], 128), "w2_down": tile_rows(f(w2_down)[0]),
        "w_in": tile_cols(w_in_p, 256), "w_out": tile_cols(f(w_out)[0], 256), "ggain": f(group_gains)[0:1],
        "relrev": f(f(rel_bias_a)[0][:, ::-1]), "t5T": f(f(t5_bias_table).T), "oh": _t5_onehot(),
        "sinks": f(sinks_b)[0:1],
    }
    cak = f(cache_a_k)[0].reshape(32, 512, 512); cav = f(cache_a_v)[0].reshape(32, 512, 512)
    cbk = f(cache_b_k)[0].reshape(32, 128, 128); cbv = f(cache_b_v)[0].reshape(32, 128, 128)
    in_maps = []
    for c in range(NCORES):
        b, q = c // 4, c % 4
        xin = np.zeros((NROWS, D), np.float32)
        if q > 0:
            xin[0:512] = x_prompt[b, q * 2048 - 512:q * 2048]
        xin[512:2560] = x_prompt[b, q * 2048:(q + 1) * 2048]
        xin[2560:2688] = x_sample[4 * c:4 * c + 4].reshape(128, D)
        cv = np.concatenate([f(c_prompt)[b:b + 1], f(c_sample)[4 * c:4 * c + 4]], axis=0)
        hn = np.full((128, 1), 0.0 if q > 0 else NEG, np.float32)
        m = dict(shared)
        m.update({"xin": xin, "cvec": f(cv), "hneg": hn, "cak": f(cak[4 * c:4 * c + 4]), "cav": f(cav[4 * c:4 * c + 4]),
                  "cbk": f(cbk[4 * c:4 * c + 4]), "cbv": f(cbv[4 * c:4 * c + 4])})
        in_maps.append(m)
    return in_maps


def _post(R):
    y_prompt = np.zeros((2, 8192, D), np.float32)
    y_sample = np.zeros((32, 32, D), np.float32)
    for c in range(NCORES):
        b, q = c // 4, c % 4
        y_prompt[b, q * 2048:(q + 1) * 2048] = R[c]["y"][0:2048]
        y_sample[4 * c:4 * c + 4] = R[c]["y"][2048:2176].reshape(4, 32, D)
    f = lambda a: np.ascontiguousarray(a, dtype=np.float32)
    nakp = np.stack([R[3]["kap"], R[7]["kap"]]).reshape(1, 2, 512, 8, 64)
    navp = np.stack([R[3]["vap"], R[7]["vap"]]).reshape(1, 2, 512, 8, 64)
    nbkp = np.stack([R[3]["kbp"], R[7]["kbp"]]).reshape(1, 2, 128, 2, 64)
    nbvp = np.stack([R[3]["vbp"], R[7]["vbp"]]).reshape(1, 2, 128, 2, 64)
    naks = np.concatenate([R[c]["kas"] for c in range(NCORES)]).reshape(1, 32, 32, 8, 64)
    navs = np.concatenate([R[c]["vas"] for c in range(NCORES)]).reshape(1, 32, 32, 8, 64)
    nbks = np.concatenate([R[c]["kbs"] for c in range(NCORES)]).reshape(1, 32, 32, 2, 64)
    nbvs = np.concatenate([R[c]["vbs"] for c in range(NCORES)]).reshape(1, 32, 32, 2, 64)
    return (y_prompt, y_sample, f(nakp), f(navp), f(nbkp), f(nbvp), f(naks), f(navs), f(nbks), f(nbvs))


_NC_CACHE = {}


def kernel(**inputs):
    in_maps = _prep(**inputs)
    if "nc" not in _NC_CACHE:
        _NC_CACHE["nc"] = build_program()
    res = run_bass_kernel_spmd(_NC_CACHE["nc"], in_maps, core_ids=list(range(NCORES)))
    return _post(res.results)
```

```python
import numpy as np
from contextlib import ExitStack
import concourse.bass as bass
import concourse.mybir as mybir
from concourse.bass_utils import run_bass_kernel_spmd

F32 = mybir.dt.float32
BF16 = mybir.dt.bfloat16
AF = mybir.ActivationFunctionType
ALU = mybir.AluOpType

NCORES = 8
D = 1024
DFF = 2816
NJ = 22
SUPS = [6, 6, 6, 4]
JS = 6
NSLOT = 4
SCALE = 0.125
EPS = 1e-6
NEG = -1e30
NROWS = 2688


class Buf:
    __slots__ = ("name", "last_w", "reads", "psum")

    def __init__(self, name, psum=False):
        self.name = name
        self.last_w = None
        self.reads = []
        self.psum = psum


class Sched:
    ENGS = ("pe", "act", "dve", "pool", "sp")

    def __init__(self, nc, es):
        self.nc = nc
        self.es = es
        self.streams = {e: [] for e in self.ENGS}
        self.sems = {}
        self.count = {}
        self.seen = {e: {} for e in self.ENGS}
        self.dead = False
        for e in self.ENGS:
            self.new_sem("E_" + e)

    def new_sem(self, name):
        s = self.es.enter_context(self.nc.semaphore(name))
        self.sems[name] = s
        self.count[name] = 0
        return name

    def _deps(self, eng, reads, writes):
        deps = {}
        own = "E_" + eng

        def add(t, psum=False):
            if t is None:
                return
            s, v = t
            if psum and s == own:
                return
            if deps.get(s, 0) < v:
                deps[s] = v
        for b in reads:
            add(b.last_w, b.psum)
            if b.psum:
                for r in b.reads:
                    add(r, True)
        for b in writes:
            add(b.last_w, b.psum)
            for r in b.reads:
                add(r, b.psum)
        waits = []
        seen = self.seen[eng]
        for s, v in deps.items():
            if eng == "pe" and s == "E_pe":
                continue
            if seen.get(s, 0) >= v:
                continue
            seen[s] = v
            waits.append((s, v))
        return waits

    def _finish(self, tok, reads, writes):
        for b in writes:
            b.last_w = tok
            b.reads = []
        for b in reads:
            if b not in writes:
                b.reads.append(tok)
                if len(b.reads) > 48:
                    m = {}
                    for s, v in b.reads:
                        if m.get(s, 0) < v:
                            m[s] = v
                    b.reads = list(m.items())

    def op(self, eng, fn, reads=(), writes=()):
        if self.dead:
            return None
        reads = [b for b in reads if b is not None]
        writes = [b for b in writes if b is not None]
        waits = self._deps(eng, reads, writes)
        sname = "E_" + eng
        self.count[sname] += 1
        tok = (sname, self.count[sname])
        self.streams[eng].append((waits, fn, sname, 1))
        self._finish(tok, reads, writes)
        return tok

    def dma(self, eng, sem, fns, reads=(), writes=()):
        if self.dead:
            return None
        reads = [b for b in reads if b is not None]
        writes = [b for b in writes if b is not None]
        waits = self._deps(eng, reads, writes)
        tok = None
        for i, fn in enumerate(fns):
            self.count[sem] += 16
            tok = (sem, self.count[sem])
            self.streams[eng].append((waits if i == 0 else [], fn, sem, 16))
        self._finish(tok, reads, writes)
        return tok

    def finish(self, eng):
        waits = [(s, v) for s, v in self.count.items() if v > 0]
        self.streams[eng].append((waits, None, None, 0))

    def build(self):
        nc = self.nc
        with nc.Block() as block:
            def mk(ename):
                def body(e):
                    for waits, fn, sname, inc in self.streams[ename]:
                        for s, v in waits:
                            e.wait_ge(self.sems[s], v)
                        if fn is not None:
                            fn(e).then_inc(self.sems[sname], inc)
                return body
            block.tensor(mk("pe"))
            block.scalar(mk("act"))
            block.vector(mk("dve"))
            block.gpsimd(mk("pool"))
            block.sync(mk("sp"))


def build_program():
    nc = bass.Bass("TRN2", target_bir_lowering=False)
    es = ExitStack()

    def din(name, shape):
        return nc.dram_tensor(name, list(shape), F32, kind="ExternalInput").ap()

    def dout(name, shape):
        return nc.dram_tensor(name, list(shape), F32, kind="ExternalOutput").ap()

    xin = din("xin", [NROWS, D])
    cvec = din("cvec", [5, D])
    hneg_d = din("hneg", [128, 1])
    ident_d = din("ident", [128, 128])
    w_mod = din("w_mod", [36, 128, 2048])
    b_mod = din("b_mod", [1, 9 * D])
    bmod_fm = din("bmod_fm", [128, 72])
    gains = din("gains", [6, D])
    gains_fm = din("gains_fm", [128, 48])
    w1g = din("w1_gate", [NJ, 128, 1024]); w1u = din("w1_up", [NJ, 128, 1024]); w1d = din("w1_down", [128, NJ, D])
    w2g = din("w2_gate", [NJ, 128, 1024]); w2u = din("w2_up", [NJ, 128, 1024]); w2d = din("w2_down", [128, NJ, D])
    w_in = din("w_in", [9, 128, 2048])
    w_out = din("w_out", [4, 128, 2048])
    ggain = din("ggain", [1, D])
    relrev = din("relrev", [8, 257])
    t5T = din("t5T", [32, 8])
    oh_d = din("oh", [32, 384])
    sinks = din("sinks", [1, 8])
    cak = din("cak", [4, 512, 512]); cav = din("cav", [4, 512, 512])
    cbk = din("cbk", [4, 128, 128]); cbv = din("cbv", [4, 128, 128])

    y_out = dout("y", [2176, D])
    kap = dout("kap", [512, 512]); vap = dout("vap", [512, 512])
    kbp = dout("kbp", [128, 128]); vbp = dout("vbp", [128, 128])
    kas = dout("kas", [128, 512]); vas = dout("vas", [128, 512])
    kbs = dout("kbs", [128, 128]); vbs = dout("vbs", [128, 128])

    g2a = nc.dram_tensor("g2a", [8, 128, 768], F32)
    g2b = nc.dram_tensor("g2b", [8, 128, 384], F32)
    wsc = nc.dram_tensor("wsc", [57, 128, 2048], BF16)
    wdsc = nc.dram_tensor("wdsc", [2, 128, NJ * D], BF16)

    with es:
        S = Sched(nc, es)

        def sb(name, shape, dt=F32):
            return es.enter_context(nc.sbuf_tensor(name, list(shape), dt))

        xr = [sb("x_res%d" % i, [128, 4, D]) for i in range(2)]; xBs = [[Buf("x%d_%d" % (j, i)) for i in range(4)] for j in range(2)]
        x_res = xr[0]; xB = xBs[0]
        X = {"t": xr[0], "B": xBs[0], "i": 0}
        y_all = sb("y_all", [128, 4, D]); yB = [Buf("y%d" % i) for i in range(4)]
        hT = sb("hT", [128, 8, 512], BF16); hBt = [[Buf("h%d_%d" % (i, t)) for t in range(4)] for i in range(8)]
        hB = [b_ for row in hBt for b_ in row]
        regA = sb("regA", [128, 8 * 512], BF16)
        aT = regA[:, :].rearrange("p (j t) -> p j t", j=8)
        aB = [Buf("a%d" % i) for i in range(8)]
        QAT = regA[:, 0:2048].rearrange("p (c t) -> p c t", c=4)
        QBT = regA[:, 2048:4096].rearrange("p (c t) -> p c t", c=4)
        qaB = aB[0:4]; qbB = aB[4:8]
        wd = sb("wd", [128, JS, D], BF16); wdB = Buf("wd")
        ring = sb("ring", [128, NSLOT, 2048], BF16); ringB = [Buf("r%d" % i) for i in range(NSLOT)]
        KAT = sb("KAT", [128, 4, 1024], BF16); kaB = [Buf("ka%d" % i) for i in range(8)]
        VA = sb("VA", [128, 8, 8, 65], BF16); vaB = [Buf("va%d" % i) for i in range(8)]
        KBT = sb("KBT", [128, 1024], BF16); kbB = [Buf("kb%d" % i) for i in range(8)]
        VB = sb("VB", [128, 8, 2, 65], BF16); vbB = [Buf("vb%d" % i) for i in range(8)]
        KnA = sb("KnA", [128, 4, 128], BF16); knaB = Buf("kna")
        KnB = sb("KnB", [128, 128], BF16); knbB = Buf("knb")
        VnA = sb("VnA", [32, 4, 8, 65], BF16); vnaB = Buf("vna")
        VnB = sb("VnB", [32, 4, 2, 65], BF16); vnbB = Buf("vnb")
        biasA = sb("biasA", [128, 5, 2, 4, 128], BF16); bAB = Buf("biasA")
        biasB = sb("biasB", [128, 2, 8, 128], BF16); bBB = Buf("biasB")
        PT = sb("PT", [128, 2, 5, 512], BF16); ptB = [[Buf("pt%d%d" % (a, k)) for k in range(5)] for a in range(2)]
        PTB = sb("PTB", [128, 2, 2, 512], BF16); ptbB = [[Buf("ptb%d%d" % (a, k)) for k in range(2)] for a in range(2)]
        sc = sb("sc", [128, 2, 512]); scB = [Buf("sc0"), Buf("sc1")]
        Cp = sb("Cp", [128, 3, D]); CpB = Buf("Cp")
        Cs = sb("Cs", [128, 3, D]); CsB = Buf("Cs")
        gg = sb("gg", [128, D]); ggB = Buf("gg")
        xn = sb("xn", [128, 2, D]); xnB = [Buf("xn0"), Buf("xn1")]
        tmp = sb("tmp", [128, D]); tmpB = Buf("tmp")
        bstg = tmp[:, :].rearrange("p (h q) -> p h q", h=8); bstgB = tmpB
        junk = sb("junk", [128, D], BF16); junkB = Buf("junk")
        sg = sb("sg", [128, 2, 512], BF16); sgB = [Buf("sg0"), Buf("sg1")]
        SpT = PT[:, 0, 0:2, :].rearrange("p a (b t) -> p (a b) t", b=4); SsT = PT[:, 1, 0:2, :].rearrange("p a (b t) -> p (a b) t", b=4)
        sptB = Buf("SpT"); sstB = Buf("SsT")
        S5 = sb("S5", [128, 8, 8], BF16); s5B = Buf("S5")
        ident = sb("ident_s", [128, 128]); identB = Buf("ident")
        identb = sb("identb", [128, 128], BF16); identbB = Buf("identb")
        kvst = sb("kvst", [128, 2, 512]); kvB = [Buf("kv0"), Buf("kv1")]
        cks = [wd[:, 0:2, :].rearrange("p a (b f) -> p (a b) f", b=2), wd[:, 3:5, :].rearrange("p a (b f) -> p (a b) f", b=2)]
        ckBs = [Buf("ck0"), Buf("ck1")]
        ckbs = [wd[:, 2, 0:128], wd[:, 2, 128:256]]; ckbBs = [Buf("ckb0"), Buf("ckb1")]
        modT = sb("modT", [128, 9, 8, 8]); modB = Buf("modT")
        Aall = sb("Aall", [128, 3, 8, 8]); AB = Buf("Aall")
        bmfm = sb("bmfm", [128, 72]); gfm = sb("gfm", [128, 48]); smallB = Buf("small")
        hneg = sb("hneg_s", [128, 1])
        eps_t = sb("eps_t", [128, 1])
        esink = sb("esink", [128, 8])
        st = sb("st", [128, 96]); stB = Buf("st")
        stpre = [Buf("stpre%d" % i) for i in range(4)]; stpost = [Buf("stpost%d" % i) for i in range(4)]
        stgn = [Buf("stgn%d" % i) for i in range(4)]
        grA_t = sb("grA_t", [8, 768]); grA = grA_t[:, :]; grB_ = sc[0:8, 0, 0:384]; grBuf = Buf("gr")
        t5s = sb("t5s", [32, 8]); ohs = sc[0:32, 1, 0:384]; relc = sb("relc", [8, 1])

        pb = [es.enter_context(nc.psum_tensor("pb%d" % i, [128, 512], F32)) for i in range(8)]
        pbB = [Buf("pb%d" % i, psum=True) for i in range(8)]
        g2aB = Buf("g2a"); g2bB = Buf("g2b")

        for nm in ("c0", "c1", "xl0", "xl1", "ys0_0", "ys0_1", "ys0_2", "ys0_3", "ys1_0", "ys1_1", "ys1_2", "ys1_3", "kv0", "kv1", "wd", "ck0", "ck1", "cv0", "cv1", "ckb0", "ckb1", "cvb0", "cvb1", "misc", "wdwb", "wdh", "bias") + tuple("r%d" % i for i in range(7)) + tuple("wb%d" % i for i in range(NSLOT)) + tuple("rh%d" % i for i in range(NSLOT)):
            S.new_sem(nm)

        cnt = {"acc": 0, "misc": 0, "kv": 0, "ys": 0, "ev": 0}

        def acc_bank():
            cnt["acc"] += 1
            return 4 + cnt["acc"] % 2

        def misc_bank():
            cnt["misc"] += 1
            return 6 + cnt["misc"] % 2

        def evac_copy(out_ap, in_ap, reads, writes, eng=None):
            if eng is None:
                cnt["ev"] += 1
                eng = "act" if cnt["ev"] % 2 else "dve"
            if eng == "act":
                S.op("act", lambda e: e.activation(out=out_ap, in_=in_ap, func=AF.Copy), reads=reads, writes=writes)
            else:
                S.op("dve", lambda e: e.tensor_copy(out=out_ap, in_=in_ap), reads=reads, writes=writes)

        def mk3(ap2d, c0, w):
            return ap2d.rearrange("(k p) n -> p k n", p=128)[:, :, c0:c0 + w]

        pieces = []
        NMOD = 36
        MSLOT = 7
        xslotB = [Buf("xs%d" % i) for i in range(3)]

        def slot_of(n):
            return (n % MSLOT) if n < NMOD else ((n - NMOD) % NSLOT)

        def slot_ap(sl):
            if sl < NSLOT:
                return ring[:, sl, :]
            return wd[:, 2 * (sl - NSLOT):2 * (sl - NSLOT) + 2, :].rearrange("p a n -> p (a n)")

        def slot_buf(sl):
            return ringB[sl] if sl < NSLOT else xslotB[sl - NSLOT]

        def add_gu(wg, wu, f):
            for j in range(NJ):
                pieces.append(("gu", wg, wu, j, ("gu", f, j)))

        for pi in range(36):
            pieces.append(("w", w_mod, pi, 256, None))
        for gi in range(6):
            add_gu(w1g, w1u, 1)
            cbs = [4, 5, 8, 2, 3] if gi == 0 else [4, 5, 8, 0, 1, 2, 3, 6, 7]
            for cb in cbs:
                pieces.append(("w", w_in, cb, 256, ("win", cb)))
            if gi > 0:
                for c in range(4):
                    pieces.append(("w", w_out, c, 256, ("wout", c)))
                add_gu(w2g, w2u, 2)
        ws = {"emitted": 0, "next": 0}
        scr_idx = {}
        scrB = {}

        def ws_emit(n):
            p = pieces[n]
            slot = slot_of(n)
            key = p[-1]
            if key is not None and key in scr_idx:
                src = wsc.ap()[scr_idx[key]]
                q, sm = ("sp", "rh%d" % slot)
                S.dma(q, sm, [lambda e, src=src, slot=slot: e.dma_start(out=ring[:, slot, :], in_=src)],
                      reads=[scrB[key]], writes=[ringB[slot]])
                return
            if p[0] == "gu":
                _, wg, wu, j, _k = p
                d0 = ring[:, slot, 0:1024]
                d1 = ring[:, slot, 1024:2048]
                s0 = wg[j]
                s1 = wu[j]
                fns = [lambda e, d0=d0, s0=s0: e.dma_start(out=d0, in_=s0),
                       lambda e, d1=d1, s1=s1: e.dma_start(out=d1, in_=s1)]
            else:
                _, w, c0, wdt, _k = p
                d0 = slot_ap(slot)
                s0 = w[c0]
                fns = [lambda e, d0=d0, s0=s0: e.dma_start(out=d0, in_=s0)]
            S.dma("pool", "r%d" % slot, fns, writes=[slot_buf(slot)])
            if key is not None:
                scr_idx[key] = len(scr_idx)
                scrB[key] = Buf("scr%d" % scr_idx[key])
                dst = wsc.ap()[scr_idx[key]]
                S.dma("sp", "wb%d" % slot, [lambda e, dst=dst, slot=slot: e.dma_start(out=dst, in_=ring[:, slot, :])],
                      reads=[ringB[slot]], writes=[scrB[key]])

        def ws_fill(base):
            lim = min(NMOD, base + MSLOT) if base < NMOD else min(len(pieces), base + NSLOT)
            while ws["emitted"] < lim:
                ws_emit(ws["emitted"])
                ws["emitted"] += 1

        def ws_next(hold_base=None):
            n = ws["next"]
            ws["next"] += 1
            ws_fill(n if hold_base is None else hold_base)
            return slot_of(n)

        c0f = [
            lambda e: e.dma_start(out=ident[:], in_=ident_d[:, :]),
            lambda e: e.dma_start(out=hneg[:], in_=hneg_d[:, :]),
            lambda e: e.dma_start(out=bmfm[:], in_=bmod_fm[:, :]),
            lambda e: e.dma_start(out=gfm[:], in_=gains_fm[:, :]),
            lambda e: e.dma_start(out=esink[:], in_=sinks[0:1, :].partition_broadcast(128)),
            lambda e: e.dma_start(out=xn[:, 0, :], in_=cvec[0:1, :].partition_broadcast(128)),
            lambda e: e.dma_start(out=t5s[:], in_=t5T[:, :]),
            lambda e: e.dma_start(out=ohs, in_=oh_d[:, :]),
            lambda e: e.dma_start(out=grA_t[:, 0:256], in_=relrev[:, 1:257]),
            lambda e: e.dma_start(out=relc[:], in_=relrev[:, 256:257], allow_slow_non_contiguous=True),
        ]
        c1f = [
            lambda e: e.dma_start(out=y_all[:, 0, :], in_=b_mod[0:1, 2048:3072].partition_broadcast(128)),
            lambda e: e.dma_start(out=y_all[:, 1, :], in_=b_mod[0:1, 5120:6144].partition_broadcast(128)),
            lambda e: e.dma_start(out=y_all[:, 2, :], in_=b_mod[0:1, 8192:9216].partition_broadcast(128)),
            lambda e: e.dma_start(out=x_res[:, 0, :], in_=gains[1:2, :].partition_broadcast(128)),
            lambda e: e.dma_start(out=x_res[:, 1, :], in_=gains[3:4, :].partition_broadcast(128)),
            lambda e: e.dma_start(out=x_res[:, 2, :], in_=gains[5:6, :].partition_broadcast(128)),
            lambda e: e.dma_start(out=gg[:], in_=ggain[0:1, :].partition_broadcast(128)),
        ]
        for b in range(4):
            c0f.append(lambda e, b=b: e.dma_start(out=xn[32 * b:32 * b + 32, 1, :],
                                                  in_=cvec[1 + b:2 + b, :].partition_broadcast(32)))
        constB = [identB, smallB, xnB[0], xnB[1], grBuf, scB[0], scB[1]]
        S.dma("sp", "c0", c0f, writes=constB)
        S.dma("act", "c1", c1f, writes=[ggB, yB[0], yB[1], yB[2], xB[0], xB[1], xB[2]])

        ws_fill(0)

        S.op("pool", lambda e: e.memset(VA[:, :, :, 64:65], 1.0), writes=vaB)
        S.op("pool", lambda e: e.memset(VB[:, :, :, 64:65], 1.0), writes=vbB)
        S.op("pool", lambda e: e.memset(VnA[:, :, :, 64:65], 1.0), writes=[vnaB])
        S.op("pool", lambda e: e.memset(VnB[:, :, :, 64:65], 1.0), writes=[vnbB])
        S.op("dve", lambda e: e.tensor_copy(out=identb[:], in_=ident[:]), reads=[identB], writes=[identbB])
        S.op("pool", lambda e: e.memset(eps_t[:], EPS), writes=[smallB])
        S.op("act", lambda e: e.activation(out=esink[:], in_=esink[:], func=AF.Exp), reads=[smallB], writes=[smallB])

        S.op("dve", lambda e: e.tensor_copy(out=grA_t[:, 256:768], in_=relc[:, 0:1].to_broadcast([8, 512])),
             reads=[grBuf], writes=[grBuf])
        S.op("pe", lambda e: e.matmul(pb[6][0:8, 0:384], lhsT=t5s[:, :], rhs=ohs, start=True, stop=True),
             reads=[grBuf, scB[1]], writes=[pbB[6]])
        S.op("dve", lambda e: e.tensor_scalar(out=grA, in0=grA, scalar1=1.0 / SCALE, scalar2=None, op0=ALU.mult),
             reads=[grBuf], writes=[grBuf])
        S.op("dve", lambda e: e.tensor_scalar(out=grB_, in0=pb[6][0:8, 0:384], scalar1=1.0 / SCALE, scalar2=None, op0=ALU.mult),
             reads=[pbB[6]], writes=[grBuf, scB[0]])
        import os
        KCUT = int(os.environ.get("KCUT", "0"))

        def cut(n):
            if KCUT == n:
                S.dead = True
        cut(1)

        def mod_phase(after8):
            for i, (dstT, dB) in enumerate(((SpT, sptB), (SsT, sstB))):
                S.op("act", lambda e, i=i: e.activation(out=xn[:, i, :], in_=xn[:, i, :], func=AF.Silu),
                     reads=[xnB[i]], writes=[xnB[i]])
                for hf in range(2):
                    bk = misc_bank()

                    def tr(e, i=i, hf=hf, bk=bk):
                        for k in range(4):
                            ins = e.transpose(out=pb[bk][:, k * 128:(k + 1) * 128],
                                              in_=xn[:, i, (hf * 4 + k) * 128:(hf * 4 + k + 1) * 128], identity=ident[:])
                        return ins
                    S.op("pe", tr, reads=[xnB[i], identB], writes=[pbB[bk]])
                    evac_copy(dstT[:, hf * 4:hf * 4 + 4, :], pb[bk][:, :].rearrange("p (k t) -> p k t", k=4),
                              [pbB[bk]], [dB])
            S.op("dve", lambda e: e.memset(S5[:], 0.0), writes=[s5B])
            S.op("dve", lambda e: e.tensor_copy(out=S5[:, :, 0:1], in_=SpT[:, :, 0:1]), reads=[sptB], writes=[s5B])
            S.op("dve", lambda e: e.tensor_copy(out=S5[:, :, 1:5], in_=SsT[:, :, 0:128:32]), reads=[sstB], writes=[s5B])
            S.op("dve", lambda e: e.memset(modT[:], 0.0), writes=[modB])

            for pi in range(36):
                if pi == 8:
                    after8()
                slot = ws_next()
                seg = pi // 4
                rs = slot_ap(slot).rearrange("p (k n) -> p k n", k=8)
                rsB = slot_buf(slot)
                if seg % 3 != 2:
                    for cc in range(2):
                        ko = (pi % 4) * 2 + cc
                        bk = misc_bank()

                        def mm(e, rs=rs, cc=cc, bk=bk):
                            for k in range(8):
                                ins = e.matmul(pb[bk][:, 0:5], lhsT=rs[:, k, cc * 128:(cc + 1) * 128], rhs=S5[:, k, 0:5],
                                               start=(k == 0), stop=(k == 7))
                            return ins
                        S.op("pe", mm, reads=[rsB, s5B], writes=[pbB[bk]])
                        S.op("dve", lambda e, seg=seg, ko=ko, bk=bk: e.tensor_scalar(
                            out=modT[:, seg, ko, 0:5], in0=pb[bk][:, 0:5], scalar1=bmfm[:, seg * 8 + ko:seg * 8 + ko + 1],
                            scalar2=None, op0=ALU.add), reads=[pbB[bk], smallB], writes=[modB])
                else:
                    sub = seg // 3
                    c0 = (pi % 4) * 256
                    for which, (ST, stB_, Ct, CB) in enumerate(((SpT, sptB, Cp, CpB), (SsT, sstB, Cs, CsB))):
                        bk = misc_bank()

                        def mm(e, rs=rs, ST=ST, bk=bk):
                            for k in range(8):
                                ins = e.matmul(pb[bk][:, 0:256], lhsT=ST[:, k, :], rhs=rs[:, k, :],
                                               start=(k == 0), stop=(k == 7))
                            return ins
                        S.op("pe", mm, reads=[rsB, stB_], writes=[pbB[bk]])
                        S.op("dve", lambda e, bk=bk, sub=sub, c0=c0: e.tensor_tensor(
                            out=tmp[:, 0:256], in0=pb[bk][:, 0:256], in1=y_all[:, sub, c0:c0 + 256], op=ALU.add),
                            reads=[pbB[bk], yB[sub]], writes=[tmpB])
                        res = 1.0 if sub == 1 else 0.5
                        S.op("dve", lambda e, Ct=Ct, sub=sub, c0=c0, res=res: e.scalar_tensor_tensor(
                            out=Ct[:, sub, c0:c0 + 256], in0=tmp[:, 0:256], scalar=res, in1=x_res[:, sub, c0:c0 + 256],
                            op0=ALU.mult, op1=ALU.mult), reads=[tmpB, xB[sub]], writes=[CB])
            for s in ():
                S.op("dve", lambda e, s=s: e.tensor_scalar(out=Aall[:, s, :, :], in0=modT[:, 3 * s + 1, :, :],
                                                           scalar1=1.0, scalar2=None, op0=ALU.add),
                     reads=[modB], writes=[AB])
                S.op("dve", lambda e, s=s: e.tensor_tensor(
                    out=Aall[:, s, :, :], in0=Aall[:, s, :, :],
                    in1=gfm[:, 16 * s:16 * s + 8].unsqueeze(2).to_broadcast([128, 8, 8]), op=ALU.mult),
                    reads=[AB, smallB], writes=[AB])

            S.dma("sp", "misc", [
                lambda e: e.dma_start(out=g2a.ap()[:, :, :], in_=grA.unsqueeze(1).to_broadcast([8, 128, 768])),
                lambda e: e.dma_start(out=g2b.ap()[:, :, :], in_=grB_.unsqueeze(1).to_broadcast([8, 128, 384])),
            ], reads=[grBuf, scB[0]], writes=[g2aB, g2bB])

        cut(2)

        def Acol(s, k, bc):
            return Aall[:, s, k, bc:bc + 1]

        def Bcol(s, k, bc):
            return modT[:, 3 * s, k, bc:bc + 1]

        def rstd_chain(n, inv_n, col0, SB, np_=128):
            a, c = col0, col0 + 2 * n
            S.op("act", lambda e: e.activation(out=st[0:np_, c:c + n], in_=st[0:np_, a:a + n], func=AF.Sqrt,
                                               scale=inv_n, bias=eps_t[0:np_, 0:1]),
                 reads=[SB, smallB], writes=[SB])
            S.op("dve", lambda e: e.reciprocal(out=st[0:np_, c:c + n], in_=st[0:np_, c:c + n]), reads=[SB], writes=[SB])
            return c

        def sumsq(src_ap, col, reads, SB, np_=128):
            S.op("act", lambda e: e.activation(out=junk[0:np_, 0:src_ap.shape[-1]], in_=src_ap, func=AF.Square,
                                               accum_out=st[0:np_, col:col + 1]),
                 reads=reads + [SB], writes=[junkB, SB])

        class Pipe:
            def __init__(self, stages_fn):
                self.fn = stages_fn
                self.live = []

            def step(self):
                for st_ in self.live:
                    if st_:
                        st_.pop(0)()
                self.live = [x for x in self.live if x]

            def push(self, lt):
                self.step()
                stg = self.fn(lt)
                stg.pop(0)()
                if stg:
                    self.live.append(stg)

            def flush(self):
                while self.live:
                    self.step()

        def norm_stages(ci, s_next, sample, final_row0, lt):
            stages = []
            xt, xb_, xidx = X["t"], X["B"], X["i"]
            if ci is not None:
                Ct, CB = (Cs, CsB) if sample else (Cp, CpB)
                SBp = stpost[lt]
                p0 = 16 + 3 * lt

                def s1():
                    S.op("dve", lambda e: e.memset(st[:, p0:p0 + 3], 0.0), reads=[SBp], writes=[SBp])
                    sumsq(y_all[:, lt, :], p0, [yB[lt]], SBp)
                    S.op("act", lambda e: e.activation(out=st[:, p0 + 2:p0 + 3], in_=st[:, p0:p0 + 1], func=AF.Sqrt,
                                                       scale=1.0 / D, bias=eps_t[:, 0:1]), reads=[SBp, smallB], writes=[SBp])

                def s2():
                    S.op("dve", lambda e: e.reciprocal(out=st[:, p0 + 2:p0 + 3], in_=st[:, p0 + 2:p0 + 3]), reads=[SBp], writes=[SBp])
                    S.op("dve", lambda e: e.scalar_tensor_tensor(
                        out=tmp[:, :], in0=y_all[:, lt, :], scalar=st[:, p0 + 2:p0 + 3], in1=Ct[:, ci, :],
                        op0=ALU.mult, op1=ALU.mult), reads=[yB[lt], SBp, CB], writes=[tmpB])
                    S.op("pool", lambda e: e.tensor_tensor(out=xt[:, lt, :], in0=xt[:, lt, :], in1=tmp[:, :], op=ALU.add),
                         reads=[xb_[lt], tmpB], writes=[xb_[lt]])
                    if final_row0 is not None:
                        r0 = final_row0 + lt * 128
                        S.dma("sp", "ys%d_%d" % (xidx, lt), [lambda e: e.dma_start(out=y_out[r0:r0 + 128, :], in_=xt[:, lt, :])],
                              reads=[xb_[lt]])
                stages += [s1, s2]
            if s_next is not None:
                SBq = stpre[lt]
                q0 = 3 * lt
                xi = lt % 2
                bks = {}

                def s3():
                    S.op("dve", lambda e: e.memset(st[:, q0:q0 + 3], 0.0), reads=[SBq], writes=[SBq])
                    sumsq(xt[:, lt, :], q0, [xb_[lt]], SBq)
                    S.op("act", lambda e: e.activation(out=st[:, q0 + 2:q0 + 3], in_=st[:, q0:q0 + 1], func=AF.Sqrt,
                                                       scale=1.0 / D, bias=eps_t[:, 0:1]), reads=[SBq, smallB], writes=[SBq])

                def s4():
                    S.op("dve", lambda e: e.reciprocal(out=st[:, q0 + 2:q0 + 3], in_=st[:, q0 + 2:q0 + 3]), reads=[SBq], writes=[SBq])
                    S.op("dve", lambda e: e.tensor_scalar(out=xn[:, xi, :], in0=xt[:, lt, :], scalar1=st[:, q0 + 2:q0 + 3],
                                                          scalar2=None, op0=ALU.mult), reads=[xb_[lt], SBq], writes=[xnB[xi]])

                def s5():
                    for hf in range(2):
                        bk = misc_bank()
                        bks[hf] = bk

                        def tr(e, hf=hf, bk=bk):
                            for k in range(4):
                                ins = e.transpose(out=pb[bk][:, k * 128:(k + 1) * 128],
                                                  in_=xn[:, xi, (hf * 4 + k) * 128:(hf * 4 + k + 1) * 128], identity=ident[:])
                            return ins
                        S.op("pe", tr, reads=[xnB[xi], identB], writes=[pbB[bk]])

                def s6():
                    for hf in range(2):
                        bk = bks[hf]
                        for kk in range(4):
                            k = hf * 4 + kk
                            blocks = [(0, 128, 0)] if not sample else [(32 * b, 32, 1 + b) for b in range(4)]
                            for (o0, w, bc) in blocks:
                                oap = hT[:, k, lt * 128 + o0:lt * 128 + o0 + w]
                                iap = pb[bk][:, kk * 128 + o0:kk * 128 + o0 + w]
                                if hf == 0:
                                    S.op("act", lambda e, oap=oap, iap=iap, k=k, bc=bc: e.activation(
                                        out=oap, in_=iap, func=AF.Identity, scale=Acol(s_next, k, bc), bias=Bcol(s_next, k, bc)),
                                        reads=[pbB[bk], AB, modB], writes=[hBt[k][lt]])
                                else:
                                    S.op("dve", lambda e, oap=oap, iap=iap, k=k, bc=bc: e.tensor_scalar(
                                        out=oap, in0=iap, scalar1=Acol(s_next, k, bc), scalar2=Bcol(s_next, k, bc),
                                        op0=ALU.mult, op1=ALU.add), reads=[pbB[bk], AB, modB], writes=[hBt[k][lt]])
                stages += [s3, s4, s5, s6]
            return stages

        def prenorm(s, nt, sample):
            P_ = Pipe(lambda lt: norm_stages(None, s, sample, None, lt))
            for lt in range(nt):
                P_.push(lt)
            P_.flush()

        wd_seen = {}

        def ffn(wdn, nt, f, after_tile=None, before_last=None, extra=None):
            T = nt * 128
            j0 = 0
            for sup, js in enumerate(SUPS):
                for jl in range(js):
                    j = j0 + jl
                    slot = ws_next()
                    gb, ub = j % 2, 2 + j % 2
                    rg = ring[:, slot, 0:1024].rearrange("p (k n) -> p k n", k=8)
                    ru = ring[:, slot, 1024:2048].rearrange("p (k n) -> p k n", k=8)

                    def mmg(e, rr=rg, bk=gb):
                        for k in range(8):
                            ins = e.matmul(pb[bk][:, 0:T], lhsT=rr[:, k, :], rhs=hT[:, k, 0:T], start=(k == 0), stop=(k == 7))
                        return ins

                    def mmu(e, rr=ru, bk=ub):
                        for k in range(8):
                            ins = e.matmul(pb[bk][:, 0:T], lhsT=rr[:, k, :], rhs=hT[:, k, 0:T], start=(k == 0), stop=(k == 7))
                        return ins
                    S.op("pe", mmg, reads=[ringB[slot]] + hB, writes=[pbB[gb]])
                    S.op("pe", mmu, reads=[ringB[slot]] + hB, writes=[pbB[ub]])
                    si = j % 2
                    S.op("act", lambda e, si=si, gb=gb: e.activation(out=sg[:, si, 0:T], in_=pb[gb][:, 0:T], func=AF.Silu),
                         reads=[pbB[gb]], writes=[sgB[si]])
                    S.op("dve", lambda e, si=si, ub=ub, jl=jl: e.tensor_tensor(out=aT[:, jl, 0:T], in0=sg[:, si, 0:T],
                                                                               in1=pb[ub][:, 0:T], op=ALU.mult),
                         reads=[sgB[si], pbB[ub]], writes=[aB[jl]])
                    if jl == 1:
                        sview = wdsc.ap()[f - 1].rearrange("p (j n) -> p j n", j=NJ)[:, j0:j0 + js, :]
                        if (f, sup) in wd_seen:
                            S.dma("sp", "wdh", [lambda e, sview=sview, js=js: e.dma_start(out=wd[:, 0:js, :], in_=sview)],
                                  reads=[wd_seen[(f, sup)]], writes=[wdB] + ckBs + ckbBs + xslotB)
                        else:
                            src = wdn[:, j0:j0 + js, :]
                            S.dma("pool", "wd", [lambda e, src=src, js=js: e.dma_start(out=wd[:, 0:js, :], in_=src)], writes=[wdB] + ckBs + ckbBs + xslotB)
                            wd_seen[(f, sup)] = Buf("wdsc%d%d" % (f, sup))
                            S.dma("sp", "wdwb", [lambda e, sview=sview, js=js: e.dma_start(out=sview, in_=wd[:, 0:js, :])],
                                  reads=[wdB], writes=[wd_seen[(f, sup)]])
                if before_last is not None and sup == len(SUPS) - 1:
                    before_last()
                for lt in range(nt):
                    for half in range(2):
                        bk = acc_bank()

                        def mmd(e, lt=lt, half=half, bk=bk, js=js):
                            for jl in range(js):
                                ins = e.matmul(pb[bk][:, 0:512], lhsT=aT[:, jl, lt * 128:(lt + 1) * 128],
                                               rhs=wd[:, jl, half * 512:(half + 1) * 512], start=(jl == 0), stop=(jl == js - 1))
                            return ins
                        S.op("pe", mmd, reads=aB[0:js] + [wdB], writes=[pbB[bk]])
                        yap = y_all[:, lt, half * 512:(half + 1) * 512]
                        if sup == 0:
                            evac_copy(yap, pb[bk][:, 0:512], [pbB[bk]], [yB[lt]])
                        else:
                            S.op("dve", lambda e, yap=yap, bk=bk: e.tensor_tensor(out=yap, in0=yap, in1=pb[bk][:, 0:512],
                                                                                  op=ALU.add),
                                 reads=[pbB[bk], yB[lt]], writes=[yB[lt]])
                    if extra is not None and sup == len(SUPS) - 1 and lt < extra[1]:
                        extra[0].push(lt)
                    if after_tile is not None and sup == len(SUPS) - 1:
                        after_tile.push(lt)
                if extra is not None and sup == len(SUPS) - 1:
                    for lt2 in range(nt, extra[1]):
                        extra[0].push(lt2)
                    extra[0].flush()
                if after_tile is not None and sup == len(SUPS) - 1:
                    after_tile.flush()
                j0 += js

        def proj(kind, ti0, nt):
            T = nt * 128
            sample = kind == "sample"
            cbs = [4, 5, 8, 2, 3] if kind == "halo" else [4, 5, 8, 0, 1, 2, 3, 6, 7]
            kcol0 = (ti0 % 8) * 128

            def fm_group(rs, cc, dst_ap, wbufs):
                bk = acc_bank()

                def mm(e, bk=bk):
                    for k in range(8):
                        ins = e.matmul(pb[bk][:, 0:T], lhsT=rs[:, k, cc * 128:(cc + 1) * 128], rhs=hT[:, k, 0:T],
                                       start=(k == 0), stop=(k == 7))
                    return ins
                S.op("pe", mm, reads=[ringB[slot]] + hB, writes=[pbB[bk]])
                evac_copy(dst_ap, pb[bk][:, 0:T], [pbB[bk]], wbufs)

            def tm_group(rs, c0, w, tcol0, m):
                bk = acc_bank()

                def mm(e, bk=bk):
                    for k in range(8):
                        ins = e.matmul(pb[bk][0:m, 0:w], lhsT=hT[:, k, tcol0:tcol0 + m], rhs=rs[:, k, c0:c0 + w],
                                       start=(k == 0), stop=(k == 7))
                    return ins
                S.op("pe", mm, reads=[ringB[slot]] + [hBt[k][tcol0 // 128] for k in range(8)], writes=[pbB[bk]])
                return bk

            for cb in cbs:
                slot = ws_next()
                rs = ring[:, slot, :].rearrange("p (k n) -> p k n", k=8)
                if cb in (0, 1):
                    for cc in range(2):
                        c = (cb % 2) * 2 + cc
                        fm_group(rs, cc, QAT[:, c, 0:T], [qaB[c]])
                elif cb in (6, 7):
                    for cc in range(2):
                        c = (cb % 2) * 2 + cc
                        fm_group(rs, cc, QBT[:, c, 0:T], [qbB[c]])
                elif cb in (2, 3):
                    for cc in range(2):
                        c = (cb % 2) * 2 + cc
                        if sample:
                            fm_group(rs, cc, KnA[:, c, 0:T], [knaB])
                        else:
                            fm_group(rs, cc, KAT[:, c, kcol0:kcol0 + T], [kaB[(ti0 + i) % 8] for i in range(nt)])
                    if kind == "last":
                        for lt in range(nt):
                            bk = tm_group(rs, 0, 256, lt * 128, 128)
                            kv_out_b(bk, 256, kap[lt * 128:(lt + 1) * 128, (cb % 2) * 256:(cb % 2) * 256 + 256])
                    if sample:
                        bk = tm_group(rs, 0, 256, 0, 128)
                        kv_out_b(bk, 256, kas[:, (cb % 2) * 256:(cb % 2) * 256 + 256])
                elif cb in (4, 5):
                    h0 = (cb % 2) * 4
                    if not sample:
                        for lt in range(nt):
                            bk = tm_group(rs, 0, 256, lt * 128, 128)
                            sl = (ti0 + lt) % 8
                            evac_copy(VA[:, sl, h0:h0 + 4, 0:64], pb[bk][:, 0:256].rearrange("p (h d) -> p h d", h=4),
                                      [pbB[bk]], [vaB[sl]])
                            if kind == "last":
                                kv_out_b(bk, 256, vap[lt * 128:(lt + 1) * 128, h0 * 64:h0 * 64 + 256])
                    else:
                        for b in range(4):
                            bk = tm_group(rs, 0, 256, 32 * b, 32)
                            evac_copy(VnA[0:32, b, h0:h0 + 4, 0:64], pb[bk][0:32, 0:256].rearrange("p (h d) -> p h d", h=4),
                                      [pbB[bk]], [vnaB])
                        bk = tm_group(rs, 0, 256, 0, 128)
                        kv_out_b(bk, 256, vas[:, h0 * 64:h0 * 64 + 256])
                else:
                    if sample:
                        fm_group(rs, 0, KnB[:, 0:T], [knbB])
                        for b in range(4):
                            bk = tm_group(rs, 0, 256, 32 * b, 32)
                            evac_copy(VnB[0:32, b, :, 0:64], pb[bk][0:32, 128:256].rearrange("p (h d) -> p h d", h=2),
                                      [pbB[bk]], [vnbB])
                        bk = tm_group(rs, 0, 256, 0, 128)
                        kv_out_b(bk, 128, kbs[:, :], 128, 0)
                        kv_out_b(bk, 128, vbs[:, :], 128, 128)
                    else:
                        for lt in range(nt):
                            bk = tm_group(rs, 0, 256, lt * 128, 128)
                            sl = (ti0 + lt) % 8
                            evac_copy(VB[:, sl, :, 0:64], pb[bk][:, 128:256].rearrange("p (h d) -> p h d", h=2),
                                      [pbB[bk]], [vbB[sl]])
                            if kind == "last" and lt == nt - 1:
                                kv_out_b(bk, 128, kbp[:, :], 128, 0)
                                kv_out_b(bk, 128, vbp[:, :], 128, 128)
                        fm_group(rs, 0, KBT[:, kcol0:kcol0 + T], [kbB[(ti0 + i) % 8] for i in range(nt)])

        def kv_out_b(bk, w, dst_ap, np_=128, c0=0):
            cnt["kv"] += 1
            ki = cnt["kv"] % 2
            S.op("dve", lambda e: e.tensor_copy(out=kvst[0:np_, ki, 0:w], in_=pb[bk][0:np_, c0:c0 + w]),
                 reads=[pbB[bk]], writes=[kvB[ki]])
            S.dma("sp", "kv%d" % ki, [lambda e: e.dma_start(out=dst_ap, in_=kvst[0:np_, ki, 0:w])], reads=[kvB[ki]])

        def attention(nq, qcol0, o_ap, oB, keyA, keyB):
            W = 4 * nq
            for i, (np_, kf, kbuf, vf, vbuf, kt, mask) in enumerate(keyA):
                X, Y = i % 2, 2 + i % 2

                def mm(e, np_=np_, kf=kf, X=X, Y=Y, kt=kt):
                    e.matmul(pb[X][0:np_, 0:W], lhsT=identb[0:np_, 0:np_], rhs=biasA[0:np_, kt, 0, :, 0:nq],
                             start=True, stop=False)
                    e.matmul(pb[Y][0:np_, 0:W], lhsT=identb[0:np_, 0:np_], rhs=biasA[0:np_, kt, 1, :, 0:nq],
                             start=True, stop=False)
                    for jh in range(4):
                        e.matmul(pb[X][0:np_, jh * nq:(jh + 1) * nq], lhsT=kf(0, jh), rhs=QAT[0:64, jh, qcol0:qcol0 + nq],
                                 start=False, stop=(jh == 3))
                        ins = e.matmul(pb[Y][0:np_, jh * nq:(jh + 1) * nq], lhsT=kf(1, jh),
                                       rhs=QAT[64:128, jh, qcol0:qcol0 + nq], start=False, stop=(jh == 3))
                    return ins
                S.op("pe", mm, reads=[kbuf, bAB, identbB] + qaB, writes=[pbB[X], pbB[Y]])
                for par, bnk in ((0, X), (1, Y)):
                    if mask:
                        S.op("act", lambda e, np_=np_, par=par, i=i, bnk=bnk: e.activation(
                            out=PT[0:np_, par, i, 0:W], in_=pb[bnk][0:np_, 0:W], func=AF.Exp, scale=SCALE,
                            bias=hneg[0:np_, 0:1]), reads=[pbB[bnk], smallB], writes=[ptB[par][i]])
                    else:
                        S.op("act", lambda e, np_=np_, par=par, i=i, bnk=bnk: e.activation(
                            out=PT[0:np_, par, i, 0:W], in_=pb[bnk][0:np_, 0:W], func=AF.Exp, scale=SCALE),
                            reads=[pbB[bnk]], writes=[ptB[par][i]])
            for i, (np_, kf, kbuf, vf, vbuf, kt) in enumerate(keyB):
                X, Y = (i + 1) % 2, 2 + (i + 1) % 2

                def mm(e, np_=np_, kf=kf, X=X, Y=Y, kt=kt):
                    e.matmul(pb[X][0:np_, 0:W], lhsT=identb[0:np_, 0:np_], rhs=biasB[0:np_, kt, 0:4, 0:nq],
                             start=True, stop=False)
                    e.matmul(pb[Y][0:np_, 0:W], lhsT=identb[0:np_, 0:np_], rhs=biasB[0:np_, kt, 4:8, 0:nq],
                             start=True, stop=False)
                    e.matmul(pb[X][0:np_, 0:W], lhsT=kf(0), rhs=QBT[0:64, :, qcol0:qcol0 + nq], start=False, stop=True)
                    return e.matmul(pb[Y][0:np_, 0:W], lhsT=kf(1), rhs=QBT[64:128, :, qcol0:qcol0 + nq],
                                    start=False, stop=True)
                S.op("pe", mm, reads=[kbuf, bBB, identbB] + qbB, writes=[pbB[X], pbB[Y]])
                for c, bnk in ((0, X), (1, Y)):
                    mask = (i == 0 and len(keyA) > 0 and keyA[3][6]) if nq == 128 else False
                    if mask:
                        S.op("act", lambda e, np_=np_, c=c, i=i, bnk=bnk: e.activation(
                            out=PTB[0:np_, c, i, 0:W], in_=pb[bnk][0:np_, 0:W], func=AF.Exp, scale=SCALE,
                            bias=hneg[0:np_, 0:1]), reads=[pbB[bnk], smallB], writes=[ptbB[c][i]])
                    else:
                        S.op("act", lambda e, np_=np_, c=c, i=i, bnk=bnk: e.activation(
                            out=PTB[0:np_, c, i, 0:W], in_=pb[bnk][0:np_, 0:W], func=AF.Exp, scale=SCALE),
                            reads=[pbB[bnk]], writes=[ptbB[c][i]])
            for par in range(2):
                ob = 4 + par

                def pv(e, par=par, ob=ob):
                    for jh in range(4):
                        h = 2 * jh + par
                        for i, (np_, kf, kbuf, vf, vbuf, kt, mask) in enumerate(keyA):
                            ins = e.matmul(pb[ob][0:nq, jh * 65:(jh + 1) * 65], lhsT=PT[0:np_, par, i, jh * nq:(jh + 1) * nq],
                                           rhs=vf(h), start=(i == 0), stop=(i == len(keyA) - 1))
                    return ins
                S.op("pe", pv, reads=ptB[par] + [k[4] for k in keyA], writes=[pbB[ob]])
            for c in range(2):
                ob = 6 + c

                def pvb(e, c=c, ob=ob):
                    for hh in range(4):
                        for i, (np_, kf, kbuf, vf, vbuf, kt) in enumerate(keyB):
                            ins = e.matmul(pb[ob][0:nq, hh * 65:(hh + 1) * 65], lhsT=PTB[0:np_, c, i, hh * nq:(hh + 1) * nq],
                                           rhs=vf(c), start=(i == 0), stop=(i == len(keyB) - 1))
                    return ins
                S.op("pe", pvb, reads=ptbB[c] + [k[4] for k in keyB], writes=[pbB[ob]])
            for par in range(2):
                ob = 4 + par
                Ov = pb[ob][0:nq, 0:260].rearrange("p (h e) -> p h e", e=65)
                S.op("dve", lambda e, Ov=Ov, par=par: e.reciprocal(out=st[0:nq, 32 + 4 * par:36 + 4 * par].unsqueeze(2),
                                                                   in_=Ov[:, :, 64:65]),
                     reads=[pbB[ob], stB], writes=[stB])
                S.op("dve", lambda e, Ov=Ov, par=par: e.tensor_tensor(
                    out=o_ap[:, 0:512].rearrange("p (j r d) -> p j r d", j=4, r=2)[:, :, par, :], in0=Ov[:, :, 0:64],
                    in1=st[0:nq, 32 + 4 * par:36 + 4 * par].unsqueeze(2).to_broadcast([nq, 4, 64]), op=ALU.mult),
                    reads=[pbB[ob], stB], writes=[oB])
            for c in range(2):
                ob = 6 + c
                Ov = pb[ob][0:nq, 0:260].rearrange("p (h e) -> p h e", e=65)
                S.op("dve", lambda e, Ov=Ov, c=c: e.tensor_tensor(out=st[0:nq, 40 + 4 * c:44 + 4 * c].unsqueeze(2),
                                                                  in0=Ov[:, :, 64:65],
                                                                  in1=esink[0:nq, 4 * c:4 * c + 4].unsqueeze(2), op=ALU.add),
                     reads=[pbB[ob], stB, smallB], writes=[stB])
                S.op("dve", lambda e, c=c: e.reciprocal(out=st[0:nq, 40 + 4 * c:44 + 4 * c],
                                                        in_=st[0:nq, 40 + 4 * c:44 + 4 * c]), reads=[stB], writes=[stB])
                S.op("dve", lambda e, Ov=Ov, c=c: e.tensor_tensor(
                    out=o_ap[:, 512 + 256 * c:768 + 256 * c].rearrange("p (h d) -> p h d", h=4), in0=Ov[:, :, 0:64],
                    in1=st[0:nq, 40 + 4 * c:44 + 4 * c].unsqueeze(2).to_broadcast([nq, 4, 64]), op=ALU.mult),
                    reads=[pbB[ob], stB], writes=[oB])

        def groupnorm_stats(o_ap, oB, nq, gi):
            SB = stgn[gi]
            col = 48 + 6 * gi
            S.op("dve", lambda e: e.memset(st[0:nq, col:col + 2], 0.0), reads=[SB], writes=[SB])
            sumsq(o_ap[:, 0:512], col, [oB], SB, nq)
            sumsq(o_ap[:, 512:1024], col + 1, [oB], SB, nq)
            return rstd_chain(2, 1.0 / 512, col, SB, nq)

        def groupnorm_apply(o_ap, oB, nq, tcol0, gi, c):
            for f_ in groupnorm_stages(o_ap, oB, nq, tcol0, gi, c):
                f_()

        def groupnorm_stages(o_ap, oB, nq, tcol0, gi, c):
            SB = stgn[gi]
            xi = gi % 2
            bks = {}

            def g1():
                for hf in range(2):
                    S.op("dve", lambda e, hf=hf: e.scalar_tensor_tensor(
                        out=xn[0:nq, xi, hf * 512:(hf + 1) * 512], in0=o_ap[:, hf * 512:(hf + 1) * 512],
                        scalar=st[0:nq, c + hf:c + hf + 1], in1=gg[0:nq, hf * 512:(hf + 1) * 512], op0=ALU.mult, op1=ALU.mult),
                        reads=[oB, SB, ggB], writes=[xnB[xi]])

            def g2():
                for hf in range(2):
                    bk = hf * 2 + (cnt["misc"] % 2)
                    cnt["misc"] += 1
                    bks[hf] = bk

                    def tr(e, hf=hf, bk=bk):
                        for k in range(4):
                            ins = e.transpose(out=pb[bk][:, k * nq:(k + 1) * nq],
                                              in_=xn[0:nq, xi, (hf * 4 + k) * 128:(hf * 4 + k + 1) * 128],
                                              identity=ident[0:nq, 0:nq])
                        return ins
                    S.op("pe", tr, reads=[xnB[xi], identB], writes=[pbB[bk]])

            def g3():
                for hf in range(2):
                    bk = bks[hf]
                    for kk in range(4):
                        k = hf * 4 + kk
                        evac_copy(hT[:, k, tcol0:tcol0 + nq], pb[bk][:, kk * nq:(kk + 1) * nq], [pbB[bk]], [hBt[k][tcol0 // 128]],
                                  eng=("act" if hf == 0 else "dve"))
            return [g1, g2, g3]

        def _groupnorm_apply_old(o_ap, oB, nq, tcol0, gi, c):
            SB = stgn[gi]
            xi = gi % 2
            for hf in range(2):
                S.op("dve", lambda e, hf=hf: e.scalar_tensor_tensor(
                    out=xn[0:nq, xi, hf * 512:(hf + 1) * 512], in0=o_ap[:, hf * 512:(hf + 1) * 512],
                    scalar=st[0:nq, c + hf:c + hf + 1], in1=gg[0:nq, hf * 512:(hf + 1) * 512], op0=ALU.mult, op1=ALU.mult),
                    reads=[oB, SB, ggB], writes=[xnB[xi]])
            for hf in range(2):
                bk = hf * 2 + (cnt["misc"] % 2)
                cnt["misc"] += 1

                def tr(e, hf=hf, bk=bk):
                    for k in range(4):
                        ins = e.transpose(out=pb[bk][:, k * nq:(k + 1) * nq],
                                          in_=xn[0:nq, xi, (hf * 4 + k) * 128:(hf * 4 + k + 1) * 128],
                                          identity=ident[0:nq, 0:nq])
                    return ins
                S.op("pe", tr, reads=[xnB[xi], identB], writes=[pbB[bk]])
                for kk in range(4):
                    k = hf * 4 + kk
                    evac_copy(hT[:, k, tcol0:tcol0 + nq], pb[bk][:, kk * nq:(kk + 1) * nq], [pbB[bk]], [hBt[k][tcol0 // 128]],
                              eng=("act" if hf == 0 else "dve"))

        def groupnorm_T(o_ap, oB, nq, tcol0, gi):
            c = groupnorm_stats(o_ap, oB, nq, gi)
            groupnorm_apply(o_ap, oB, nq, tcol0, gi, c)

        def wout_stage(lt, slots):
            for c in range(4):
                slot = slots[c]
                rs = ring[:, slot, :].rearrange("p (k n) -> p k n", k=8)
                bk = acc_bank()

                def mm(e, bk=bk, rs=rs):
                    for k in range(8):
                        ins = e.matmul(pb[bk][:, 0:256], lhsT=hT[:, k, lt * 128:(lt + 1) * 128], rhs=rs[:, k, :],
                                       start=(k == 0), stop=(k == 7))
                    return ins
                S.op("pe", mm, reads=[ringB[slot]] + [hBt[k][lt] for k in range(8)], writes=[pbB[bk]])
                evac_copy(y_all[:, lt, c * 256:(c + 1) * 256], pb[bk][:, 0:256], [pbB[bk]], [yB[lt]])

        def wout_proj(nt, sample, pre=None):
            base = ws["next"]
            slots = [ws_next(hold_base=base) for _ in range(4)]

            def stages(lt):
                stg = list(pre(lt)) if pre is not None else []
                stg.append(lambda: wout_stage(lt, slots))
                return stg + norm_stages(1, 2, sample, None, lt)
            P_ = Pipe(stages)
            for lt in range(nt):
                P_.push(lt)
            P_.flush()
            ws_fill(ws["next"] - 1)

        def load_x(row0, nt, bi):
            src = xin[row0:row0 + nt * 128, :].rearrange("(t p) d -> p t d", p=128)
            S.dma("sp", "xl%d" % bi, [lambda e: e.dma_start(out=xr[bi][:, 0:nt, :], in_=src)], writes=xBs[bi][0:nt])

        def use_x(bi):
            X["t"], X["B"], X["i"] = xr[bi], xBs[bi], bi

        def prompt_keys(ti):
            keyA = []
            for kt in range(5):
                tk = ti - 4 + kt
                sl = tk % 8
                keyA.append((128,
                             (lambda half, c, sl=sl: KAT[64 * half:64 * half + 64, c, sl * 128:(sl + 1) * 128]),
                             kaB[sl], (lambda h, sl=sl: VA[:, sl, h, :]), vaB[sl], kt, tk < 4))
            keyB = []
            for kt in range(2):
                tk = ti - 1 + kt
                sl = tk % 8
                keyB.append((128, (lambda c, sl=sl: KBT[64 * c:64 * c + 64, sl * 128:(sl + 1) * 128]), kbB[sl],
                             (lambda c, sl=sl: VB[:, sl, c, :]), vbB[sl], kt))
            return keyA, keyB

        def chain(ci, s_next, sample, final_row0=None):
            return Pipe(lambda lt: norm_stages(ci, s_next, sample, final_row0, lt))

        def finalize_ab(slist):
            for s_ in slist:
                S.op("dve", lambda e, s_=s_: e.tensor_scalar(out=Aall[:, s_, :, :], in0=modT[:, 3 * s_ + 1, :, :],
                                                             scalar1=1.0, scalar2=None, op0=ALU.add),
                     reads=[modB], writes=[AB])
                S.op("dve", lambda e, s_=s_: e.tensor_tensor(
                    out=Aall[:, s_, :, :], in0=Aall[:, s_, :, :],
                    in1=gfm[:, 16 * s_:16 * s_ + 8].unsqueeze(2).to_broadcast([128, 8, 8]), op=ALU.mult),
                    reads=[AB, smallB], writes=[AB])

        def halo_start():
            finalize_ab([0])
            load_x(0, 4, 1)
            use_x(1)
            prenorm(0, 4, False)

        mod_phase(halo_start)
        finalize_ab([1, 2])
        use_x(1)
        cut(3)
        load_x(4 * 128, 4, 0)
        ffn(w1d, 4, 1, after_tile=chain(0, 1, False))
        cut(4)
        tfn = []
        for kt in range(5):
            for par in range(2):
                src = bass.AP(g2a, 639 - 128 * kt + par * 98304, [[767, 128], [2 * 98304, 4], [1, 128]])
                tfn.append(lambda e, src=src, kt=kt, par=par: e.dma_start(out=biasA[:, kt, par, :, :], in_=src))
        for kt in range(2):
            src = bass.AP(g2b, 255 - 128 * kt, [[383, 128], [49152, 8], [1, 128]])
            tfn.append(lambda e, src=src, kt=kt: e.dma_start(out=biasB[:, kt, :, :], in_=src))
        S.dma("pool", "bias", tfn, reads=[g2aB, g2bB], writes=[bAB, bBB])
        proj("halo", 0, 4)
        cut(5)
        use_x(0)
        prenorm(0, 4, False)
        for g in range(4):
            ti0 = 4 + 4 * g
            bi = g % 2
            use_x(bi)
            ffn(w1d, 4, 1, after_tile=chain(0, 1, False))
            if g < 3:
                load_x((ti0 + 4) * 128, 4, 1 - bi)
            else:
                load_x(2560, 1, 1 - bi)
            proj("last" if g == 3 else "main", ti0, 4)
            cut(6)
            if g == 0:
                S.op("dve", lambda e: e.memset(biasA[0:64, 0, :, :, 64:128], NEG), writes=[bAB])
                S.op("dve", lambda e: e.memset(biasA[64:128, 4, :, :, 0:64], NEG), writes=[bAB])
                S.op("dve", lambda e: e.memset(biasB[0:64, 0, :, 64:128], NEG), writes=[bBB])
                S.op("dve", lambda e: e.memset(biasB[64:128, 1, :, 0:64], NEG), writes=[bBB])
            for lt in range(4):
                keyA, keyB = prompt_keys(ti0 + lt)
                attention(128, lt * 128, y_all[:, lt, :], yB[lt], keyA, keyB)
            cut(7)
            cut(8)

            def gn_pre(lt):
                return ([lambda lt=lt: groupnorm_stats(y_all[:, lt, :], yB[lt], 128, lt)]
                        + groupnorm_stages(y_all[:, lt, :], yB[lt], 128, lt * 128, lt, 48 + 6 * lt + 4))
            wout_proj(4, False, pre=gn_pre)
            cut(9)

            def next_stages(lt, g=g, bi=bi):
                use_x(1 - bi)
                stg = norm_stages(None, 0, g == 3, None, lt)
                use_x(bi)
                return stg
            ffn(w2d, 4, 2, after_tile=chain(2, None, False, final_row0=g * 512),
                extra=(Pipe(next_stages), 4 if g < 3 else 1))
            cut(10 + g)

        use_x(0)
        ffn(w1d, 1, 1, after_tile=chain(0, 1, True))
        proj("sample", 0, 1)
        cut(14)
        for b in range(4):
            bb = b % 2
            ck, ckB, ckb, ckbB = cks[bb], ckBs[bb], ckbs[bb], ckbBs[bb]
            S.dma("pool", "ck%d" % bb, [lambda e, b=b, ck=ck: e.dma_start(out=ck,
                                                                        in_=cak[b].rearrange("(t p) f -> p t f", p=128))],
                  writes=[ckB, wdB])
            for c in range(4):
                bk = c % 4

                def tr(e, c=c, bk=bk, ck=ck):
                    pbv = pb[bk][:, :].bitcast(BF16)
                    for kt in range(4):
                        ins = e.transpose(out=pbv[:, kt * 128:(kt + 1) * 128], in_=ck[:, kt, c * 128:(c + 1) * 128],
                                          identity=identb[:])
                    return ins
                S.op("pe", tr, reads=[ckB, identbB], writes=[pbB[bk]])
                evac_copy(KAT[:, c, bb * 512:(bb + 1) * 512], pb[bk][:, :].bitcast(BF16)[:, 0:512], [pbB[bk]],
                          [kaB[bb * 4 + i] for i in range(4)])
            S.dma("pool", "cv%d" % bb, [lambda e, b=b, bb=bb, kt=kt: e.dma_start(
                out=VA[:, bb * 4 + kt, :, 0:64], in_=cav[b][kt * 128:(kt + 1) * 128, :].rearrange("p (h d) -> p h d", h=8))
                for kt in range(4)], writes=[vaB[bb * 4 + i] for i in range(4)])
            S.dma("pool", "ckb%d" % bb, [lambda e, b=b, ckb=ckb: e.dma_start(out=ckb, in_=cbk[b])], writes=[ckbB, wdB])

            def trb(e, ckb=ckb):
                pbv = pb[0][:, :].bitcast(BF16)
                return e.transpose(out=pbv[:, 0:128], in_=ckb, identity=identb[:])
            S.op("pe", trb, reads=[ckbB, identbB], writes=[pbB[0]])
            evac_copy(KBT[:, bb * 128:(bb + 1) * 128], pb[0][:, :].bitcast(BF16)[:, 0:128], [pbB[0]], [kbB[bb]])
            S.dma("pool", "cvb%d" % bb, [lambda e, b=b, bb=bb: e.dma_start(
                out=VB[:, bb, :, 0:64], in_=cbv[b].rearrange("p (h d) -> p h d", h=2))], writes=[vbB[bb]])
            keyA = []
            for kt in range(4):
                sl = bb * 4 + kt
                keyA.append((128,
                             (lambda half, c, sl=sl: KAT[64 * half:64 * half + 64, c, sl * 128:(sl + 1) * 128]),
                             kaB[sl], (lambda h, sl=sl: VA[:, sl, h, :]), vaB[sl], kt, False))
            keyA.append((32, (lambda half, c, b=b: KnA[64 * half:64 * half + 64, c, 32 * b:32 * b + 32]), knaB,
                         (lambda h, b=b: VnA[0:32, b, h, :]), vnaB, 4, False))
            keyB = [(128, (lambda c, bb=bb: KBT[64 * c:64 * c + 64, bb * 128:(bb + 1) * 128]), kbB[bb],
                     (lambda c, bb=bb: VB[:, bb, c, :]), vbB[bb], 0),
                    (32, (lambda c, b=b: KnB[64 * c:64 * c + 64, 32 * b:32 * b + 32]), knbB,
                     (lambda c, b=b: VnB[0:32, b, c, :]), vnbB, 1)]
            attention(32, 32 * b, y_all[0:32, b, :], yB[b], keyA, keyB)
        gcs = [groupnorm_stats(y_all[0:32, b, :], yB[b], 32, b) for b in range(4)]
        for b in range(4):
            groupnorm_apply(y_all[0:32, b, :], yB[b], 32, 32 * b, b, gcs[b])
        cut(15)
        wout_proj(1, True)
        ffn(w2d, 1, 2, after_tile=chain(2, None, True, final_row0=2048))

        assert S.dead or ws["next"] == len(pieces), (ws["next"], len(pieces))
        S.finish("sp")
        import os
        if os.environ.get("KSTATS"):
            print({e: len(v) for e, v in S.streams.items()}, S.count)
        S.build()
    return nc


def _t5_onehot():
    oh = np.zeros((32, 384), np.float32)
    for m in range(383):
        rel = 127 - m
        n = abs(rel)
        base = 16 if rel > 0 else 0
        if n < 8:
            bk = n
        else:
            bk = min(8 + ((n * n) // 64).bit_length() - 1, 15)
        oh[base + bk, m] = 1.0
    return oh


def _prep(x_prompt, x_sample, cache_a_k, cache_a_v, cache_b_k, cache_b_v, c_prompt, c_sample,
          w_mod, b_mod, norm_gains, w1_gate, w1_up, w1_down, w_in, w_out, group_gains,
          rel_bias_a, t5_bias_table, sinks_b, w2_gate, w2_up, w2_down):
    f = lambda a: np.ascontiguousarray(np.asarray(a, dtype=np.float32))
    x_prompt = f(x_prompt); x_sample = f(x_sample)
    perm = list(range(1536))
    for cc in range(4):
        perm += list(range(1536 + 64 * cc, 1536 + 64 * cc + 64))
        perm += list(range(1536 + 64 * (4 + cc), 1536 + 64 * (4 + cc) + 64))
    perm += list(range(2048, 2304))
    w_in_p = f(np.asarray(w_in)[0][:, perm])
    g6 = f(norm_gains)[0]
    def tile_cols(w, cw):
        n = w.shape[1] // cw
        return f(w.reshape(8, 128, n, cw).transpose(2, 1, 0, 3).reshape(n, 128, 8 * cw))

    def tile_rows(w):
        return f(w.reshape(NJ, 128, D).transpose(1, 0, 2))
    shared = {
        "ident": np.eye(128, dtype=np.float32),
        "w_mod": tile_cols(f(w_mod)[0], 256), "b_mod": f(b_mod)[0:1],
        "bmod_fm": f(f(b_mod)[0].reshape(72, 128).T),
        "gains": g6, "gains_fm": f(g6.reshape(6, 8, 128).transpose(2, 0, 1).reshape(128, 48)),
        "w1_gate": tile_cols(f(w1_gate)[0], 128), "w1_up": tile_cols(f(w1_up)[0], 128), "w1_down": tile_rows(f(w1_down)[0]),
        "w2_gate": tile_cols(f(w2_gate)[0], 128), "w2_up": tile_cols(f(w2_up)[0], 128), "w2_down": tile_rows(f(w2_down)[0]),
        "w_in": tile_cols(w_in_p, 256), "w_out": tile_cols(f(w_out)[0], 256), "ggain": f(group_gains)[0:1],
        "relrev": f(f(rel_bias_a)[0][:, ::-1]), "t5T": f(f(t5_bias_table).T), "oh": _t5_onehot(),
        "sinks": f(sinks_b)[0:1],
    }
    cak = f(cache_a_k)[0].reshape(32, 512, 512); cav = f(cache_a_v)[0].reshape(32, 512, 512)
    cbk = f(cache_b_k)[0].reshape(32, 128, 128); cbv = f(cache_b_v)[0].reshape(32, 128, 128)
    in_maps = []
    for c in range(NCORES):
        b, q = c // 4, c % 4
        xin = np.zeros((NROWS, D), np.float32)
        if q > 0:
            xin[0:512] = x_prompt[b, q * 2048 - 512:q * 2048]
        xin[512:2560] = x_prompt[b, q * 2048:(q + 1) * 2048]
        xin[2560:2688] = x_sample[4 * c:4 * c + 4].reshape(128, D)
        cv = np.concatenate([f(c_prompt)[b:b + 1], f(c_sample)[4 * c:4 * c + 4]], axis=0)
        hn = np.full((128, 1), 0.0 if q > 0 else NEG, np.float32)
        m = dict(shared)
        m.update({"xin": xin, "cvec": f(cv), "hneg": hn, "cak": f(cak[4 * c:4 * c + 4]), "cav": f(cav[4 * c:4 * c + 4]),
                  "cbk": f(cbk[4 * c:4 * c + 4]), "cbv": f(cbv[4 * c:4 * c + 4])})
        in_maps.append(m)
    return in_maps


def _post(R):
    y_prompt = np.zeros((2, 8192, D), np.float32)
    y_sample = np.zeros((32, 32, D), np.float32)
    for c in range(NCORES):
        b, q = c // 4, c % 4
        y_prompt[b, q * 2048:(q + 1) * 2048] = R[c]["y"][0:2048]
        y_sample[4 * c:4 * c + 4] = R[c]["y"][2048:2176].reshape(4, 32, D)
    f = lambda a: np.ascontiguousarray(a, dtype=np.float32)
    nakp = np.stack([R[3]["kap"], R[7]["kap"]]).reshape(1, 2, 512, 8, 64)
    navp = np.stack([R[3]["vap"], R[7]["vap"]]).reshape(1, 2, 512, 8, 64)
    nbkp = np.stack([R[3]["kbp"], R[7]["kbp"]]).reshape(1, 2, 128, 2, 64)
    nbvp = np.stack([R[3]["vbp"], R[7]["vbp"]]).reshape(1, 2, 128, 2, 64)
    naks = np.concatenate([R[c]["kas"] for c in range(NCORES)]).reshape(1, 32, 32, 8, 64)
    navs = np.concatenate([R[c]["vas"] for c in range(NCORES)]).reshape(1, 32, 32, 8, 64)
    nbks = np.concatenate([R[c]["kbs"] for c in range(NCORES)]).reshape(1, 32, 32, 2, 64)
    nbvs = np.concatenate([R[c]["vbs"] for c in range(NCORES)]).reshape(1, 32, 32, 2, 64)
    return (y_prompt, y_sample, f(nakp), f(navp), f(nbkp), f(nbvp), f(naks), f(navs), f(nbks), f(nbvs))


_NC_CACHE = {}


def kernel(**inputs):
    in_maps = _prep(**inputs)
    if "nc" not in _NC_CACHE:
        _NC_CACHE["nc"] = build_program()
    res = run_bass_kernel_spmd(_NC_CACHE["nc"], in_maps, core_ids=list(range(NCORES)))
    return _post(res.results)
```
